# Optimizing a Trainium2 kernel written in Bass

```python
import math
import jax
import jax.numpy as jnp
from jax import lax
import numpy as np

D_MODEL = 2048
BATCH = 16
SEQ = 256
DEPTH = 2
DEC_BATCH = 2
DEC_SEQ = 4096
PAST_LEN = 512

GRID_W = 64
W_A = 1024
NA_HEADS = 8
NA_HEAD_DIM = 128
W_B = NA_HEADS * NA_HEAD_DIM
W_C = 1024
W_D = 1024
NA_WIN_R = 8
NA_WIN_C = 16
SHORT_K = 3
CONF_K = 31
D_FF = 5632
HY_EMB = 33
HY_HIDDEN = 64
HY_FAST_PCT = 0.3
HY_SLOW_PCT = 1.5
HY_TARGET = 1e-2
Q_BLOCK = 128
EPS = 1e-6
NEG_INF = -1e30
N_EVEN = (DEPTH + 1) // 2
N_ODD = DEPTH // 2

kernel_name = "hybrid_diffusion_prefix_step"


def _rmsnorm(x, g):
    xf = x.astype(jnp.float32)
    y = xf * lax.rsqrt(jnp.mean(xf * xf, axis=-1, keepdims=True) + EPS)
    return (y * g.astype(jnp.float32)).astype(x.dtype)


def _layernorm(x, g, b):
    xf = x.astype(jnp.float32)
    mu = jnp.mean(xf, axis=-1, keepdims=True)
    var = jnp.mean(jnp.square(xf - mu), axis=-1, keepdims=True)
    y = (xf - mu) * lax.rsqrt(var + EPS)
    return (y * g.astype(jnp.float32) + b.astype(jnp.float32)).astype(x.dtype)


def _dwconv(x, w, b=None):
    k, ch = w.shape
    y = lax.conv_general_dilated(x, w[:, None, :].astype(x.dtype), (1,), [(k // 2, k // 2)],
                                 dimension_numbers=("NWC", "WIO", "NWC"), feature_group_count=ch)
    if b is not None:
        y = y + b.astype(x.dtype)
    return y


def _adaln(cvec, w, b):
    m = jax.nn.silu(cvec) @ w + b
    return m.reshape(cvec.shape[0], 6, D_MODEL)


def _modulate(h, shift, scale):
    return h * (1 + scale[:, None, :]) + shift[:, None, :]


def _ctx_attention(q, k, v):
    bsz, L, H, Dh = q.shape
    scale = Dh ** -0.5
    qb = q.reshape(bsz, L // Q_BLOCK, Q_BLOCK, H, Dh).transpose(1, 0, 2, 3, 4)

    def blk(qi):
        s = jnp.einsum("bqhd,bkhd->bhqk", qi, k, preferred_element_type=jnp.float32) * scale
        p = jax.nn.softmax(s, axis=-1).astype(v.dtype)
        return jnp.einsum("bhqk,bkhd->bqhd", p, v)

    o = lax.map(blk, qb)
    return o.transpose(1, 0, 2, 3, 4).reshape(bsz, L, H * Dh)


def _neighbourhood_attention(q, k, v, ctx_k, ctx_v, rpb):
    bsz, L, H, Dh = q.shape
    rows = L // GRID_W
    wr = min(NA_WIN_R, rows)
    qc = NA_WIN_C
    n_cb = GRID_W // qc
    slab = 2 * qc
    n_loc = wr * slab
    scale = Dh ** -0.5
    kg = k.reshape(bsz, rows, GRID_W, H, Dh)
    vg = v.reshape(bsz, rows, GRID_W, H, Dh)
    q_rows = q.reshape(bsz, rows, n_cb, qc, H, Dh).transpose(1, 0, 2, 3, 4, 5)
    r_idx = jnp.arange(rows)
    row_start = jnp.clip(r_idx - wr // 2, 0, rows - wr)
    cols = jnp.arange(GRID_W).reshape(n_cb, qc)
    col_start = jnp.clip(cols - qc // 2, 0, GRID_W - qc)
    slab_cols = (jnp.clip(jnp.arange(n_cb) * qc - qc // 2, 0, GRID_W - slab)[:, None]
                 + jnp.arange(slab))
    rel = slab_cols[:, None, :] - col_start[..., None]
    col_ok = (rel >= 0) & (rel < qc)
    dc_idx = jnp.clip(slab_cols[:, None, :] - cols[..., None] + NA_WIN_C - 1, 0, 2 * NA_WIN_C - 2)

    def one_row(args):
        q_r, r0, r = args
        k_band = lax.dynamic_slice_in_dim(kg, r0, wr, axis=1)[:, :, slab_cols]
        v_band = lax.dynamic_slice_in_dim(vg, r0, wr, axis=1)[:, :, slab_cols]
        s_loc = jnp.einsum("bnqhd,brnshd->bhnqrs", q_r, k_band, preferred_element_type=jnp.float32) * scale
        dr_idx = r0 + jnp.arange(wr) - r + NA_WIN_R - 1
        bias = rpb[:, dr_idx[None, None, :, None], dc_idx[:, :, None, :]].astype(jnp.float32)
        s_loc = jnp.where(col_ok[:, :, None, :], s_loc + bias, NEG_INF)
        s_ctx = jnp.einsum("bnqhd,bkhd->bhnqk", q_r, ctx_k, preferred_element_type=jnp.float32) * scale
        s = jnp.concatenate([s_loc.reshape(bsz, H, n_cb, qc, n_loc), s_ctx], axis=-1)
        p = jax.nn.softmax(s, axis=-1).astype(v.dtype)
        p_loc = p[..., :n_loc].reshape(bsz, H, n_cb, qc, wr, slab)
        return (jnp.einsum("bhnqrs,brnshd->bnqhd", p_loc, v_band)
                + jnp.einsum("bhnqk,bkhd->bnqhd", p[..., n_loc:], ctx_v))

    o = lax.map(one_row, (q_rows, row_start, r_idx))
    return o.transpose(1, 0, 2, 3, 4, 5).reshape(bsz, L, H * Dh)


def _even_mixer(h, w_in, conv_a, q_norm, k_norm, rpb, w_out, ctx_kv):
    bsz, L, _ = h.shape
    a_b, a_c, a_x, q, k, v = jnp.split(
        h @ w_in, [W_A, 2 * W_A, 3 * W_A, 3 * W_A + W_B, 3 * W_A + 2 * W_B], axis=-1)
    y_a = a_b * _dwconv(a_c * a_x, conv_a)
    q = _rmsnorm(q.reshape(bsz, L, NA_HEADS, NA_HEAD_DIM), q_norm)
    k = _rmsnorm(k.reshape(bsz, L, NA_HEADS, NA_HEAD_DIM), k_norm)
    v = v.reshape(bsz, L, NA_HEADS, NA_HEAD_DIM)
    if ctx_kv is None:
        y_b = _ctx_attention(q, k, v)
        kv = (k, v)
    else:
        y_b = _neighbourhood_attention(q, k, v, ctx_kv[0], ctx_kv[1], rpb)
        kv = None
    return jnp.concatenate([y_a, y_b], axis=-1) @ w_out, kv


def _hyena_filter(L, w1, b1, f1, w2, b2, f2, w3):
    f32 = jnp.float32
    t = jnp.linspace(0.0, 1.0, L, dtype=f32)[:, None]
    bands = (HY_EMB - 1) // 2
    ang = 2 * math.pi * jnp.arange(L, dtype=f32)[:, None] / L
    freqs = jnp.linspace(1e-4, bands - 1, bands, dtype=f32)[None, :]
    z = jnp.concatenate([t, jnp.cos(freqs * ang), -jnp.sin(freqs * ang)], axis=-1)
    hid = jnp.sin(f1.astype(f32) * (z @ w1.astype(f32) + b1.astype(f32)))
    hid = jnp.sin(f2.astype(f32) * (hid @ w2.astype(f32) + b2.astype(f32)))
    hf = (hid @ w3.astype(f32)).reshape(L, 2, W_D)
    max_decay = math.log(HY_TARGET) / HY_FAST_PCT
    min_decay = math.log(HY_TARGET) / HY_SLOW_PCT
    deltas = jnp.linspace(min_decay, max_decay, W_D, dtype=f32)
    hf = hf * jnp.exp(-t * jnp.abs(deltas))[:, None, :]
    fwd, bwd = hf[:, 0], hf[:, 1]
    return jnp.concatenate([fwd, jnp.zeros((1, W_D), f32), bwd[:0:-1]], axis=0)


def _bidir_fftconv(u, filt, bias):
    L = u.shape[1]
    uf = u.astype(jnp.float32)
    spec = jnp.fft.rfft(uf, n=2 * L, axis=1) * jnp.fft.rfft(filt, n=2 * L, axis=0)[None]
    y = jnp.fft.irfft(spec, n=2 * L, axis=1)[:, :L]
    return (y + uf * bias.astype(jnp.float32)).astype(u.dtype)


def _odd_mixer(h, w_in, conf_dw, conf_dw_b, conf_ln_g, conf_ln_b, hy_short, hy_short_b,
               hy_w1, hy_b1, hy_f1, hy_w2, hy_b2, hy_f2, hy_w3, hy_bias, w_out):
    L = h.shape[1]
    c_a, c_g, hy = jnp.split(h @ w_in, [W_C, 2 * W_C], axis=-1)
    u = c_a * jax.nn.sigmoid(c_g)
    u = jax.nn.silu(_layernorm(_dwconv(u, conf_dw, conf_dw_b), conf_ln_g, conf_ln_b))
    x0, x1, vv = jnp.split(_dwconv(hy, hy_short, hy_short_b), [W_D, 2 * W_D], axis=-1)
    filt = _hyena_filter(L, hy_w1, hy_b1, hy_f1, hy_w2, hy_b2, hy_f2, hy_w3)
    z = x0 * _bidir_fftconv(vv * x1, filt, hy_bias)
    return jnp.concatenate([u, z], axis=-1) @ w_out


def _conv_ffn(h, w_in, conv, w_out):
    a, g = jnp.split(h @ w_in, 2, axis=-1)
    return (jax.nn.gelu(_dwconv(a, conv)) * g) @ w_out


def setup_inputs(seed: int = 0) -> dict:
    key = jax.random.key(seed)
    keys = iter(jax.random.split(key, 64))

    def nrm(shape, std):
        return jax.random.normal(next(keys), shape, jnp.float32) * std

    def gain(shape):
        return 1.0 + nrm(shape, 0.01)

    hd = NA_HEAD_DIM
    return {
        "x_prompt": nrm((BATCH, SEQ, D_MODEL), 1.0),
        "x_sample": nrm((DEC_BATCH, DEC_SEQ, D_MODEL), 1.0),
        "cache_k": nrm((DEC_BATCH, N_EVEN, PAST_LEN, NA_HEADS, hd), 1.0),
        "cache_v": nrm((DEC_BATCH, N_EVEN, PAST_LEN, NA_HEADS, hd), 1.0),
        "c": nrm((DEC_BATCH, D_MODEL), 1.0),
        "c_ctx": nrm((D_MODEL,), 1.0),
        "ada_w": nrm((DEPTH, D_MODEL, 6 * D_MODEL), 0.5 * D_MODEL ** -0.5),
        "ada_b": nrm((DEPTH, 6 * D_MODEL), 0.02),
        "norm_mix": gain((DEPTH, D_MODEL)),
        "norm_ffn": gain((DEPTH, D_MODEL)),
        "e_w_in": nrm((N_EVEN, D_MODEL, 3 * W_A + 3 * W_B), D_MODEL ** -0.5),
        "e_conv_a": nrm((N_EVEN, SHORT_K, W_A), SHORT_K ** -0.5),
        "e_q_norm": gain((N_EVEN, hd)),
        "e_k_norm": gain((N_EVEN, hd)),
        "e_rpb": nrm((N_EVEN, NA_HEADS, 2 * NA_WIN_R - 1, 2 * NA_WIN_C - 1), 0.02),
        "e_w_out": nrm((N_EVEN, W_A + W_B, D_MODEL), (W_A + W_B) ** -0.5),
        "o_w_in": nrm((N_ODD, D_MODEL, 2 * W_C + 3 * W_D), D_MODEL ** -0.5),
        "o_conf_dw": nrm((N_ODD, CONF_K, W_C), CONF_K ** -0.5),
        "o_conf_dw_b": nrm((N_ODD, W_C), 0.02),
        "o_conf_ln_g": gain((N_ODD, W_C)),
        "o_conf_ln_b": nrm((N_ODD, W_C), 0.02),
        "o_hy_short": nrm((N_ODD, SHORT_K, 3 * W_D), SHORT_K ** -0.5),
        "o_hy_short_b": nrm((N_ODD, 3 * W_D), 0.02),
        "o_hy_w1": nrm((N_ODD, HY_EMB, HY_HIDDEN), HY_EMB ** -0.5),
        "o_hy_b1": nrm((N_ODD, HY_HIDDEN), 0.02),
        "o_hy_f1": gain((N_ODD, HY_HIDDEN)),
        "o_hy_w2": nrm((N_ODD, HY_HIDDEN, HY_HIDDEN), HY_HIDDEN ** -0.5),
        "o_hy_b2": nrm((N_ODD, HY_HIDDEN), 0.02),
        "o_hy_f2": gain((N_ODD, HY_HIDDEN)),
        "o_hy_w3": nrm((N_ODD, HY_HIDDEN, 2 * W_D), 0.1 * HY_HIDDEN ** -0.5),
        "o_hy_bias": nrm((N_ODD, W_D), 1.0),
        "o_w_out": nrm((N_ODD, W_C + W_D, D_MODEL), (W_C + W_D) ** -0.5),
        "ffn_in": nrm((DEPTH, D_MODEL, 2 * D_FF), D_MODEL ** -0.5),
        "ffn_conv": nrm((DEPTH, SHORT_K, D_FF), SHORT_K ** -0.5),
        "ffn_out": nrm((DEPTH, D_FF, D_MODEL), D_FF ** -0.5),
    }


def reference(x_prompt, x_sample, cache_k, cache_v, c, c_ctx, ada_w, ada_b, norm_mix, norm_ffn,
              e_w_in, e_conv_a, e_q_norm, e_k_norm, e_rpb, e_w_out,
              o_w_in, o_conf_dw, o_conf_dw_b, o_conf_ln_g, o_conf_ln_b, o_hy_short, o_hy_short_b,
              o_hy_w1, o_hy_b1, o_hy_f1, o_hy_w2, o_hy_b2, o_hy_f2, o_hy_w3, o_hy_bias, o_w_out,
              ffn_in, ffn_conv, ffn_out):
    xp, xs = x_prompt, x_sample
    ks_new, vs_new = [], []
    for layer in range(DEPTH):
        j = layer // 2
        mod_p = _adaln(c_ctx[None, :], ada_w[layer], ada_b[layer])
        mod_s = _adaln(c, ada_w[layer], ada_b[layer])
        hp = _modulate(_rmsnorm(xp, norm_mix[layer]), mod_p[:, 0], mod_p[:, 1])
        hs = _modulate(_rmsnorm(xs, norm_mix[layer]), mod_s[:, 0], mod_s[:, 1])
        if layer % 2 == 0:
            ev = (e_w_in[j], e_conv_a[j], e_q_norm[j], e_k_norm[j], e_rpb[j], e_w_out[j])
            mp, (kp, vp) = _even_mixer(hp, *ev, None)
            ms, _ = _even_mixer(hs, *ev, (cache_k[:, j], cache_v[:, j]))
            ks_new.append(kp)
            vs_new.append(vp)
        else:
            od = (o_w_in[j], o_conf_dw[j], o_conf_dw_b[j], o_conf_ln_g[j], o_conf_ln_b[j],
                  o_hy_short[j], o_hy_short_b[j], o_hy_w1[j], o_hy_b1[j], o_hy_f1[j],
                  o_hy_w2[j], o_hy_b2[j], o_hy_f2[j], o_hy_w3[j], o_hy_bias[j], o_w_out[j])
            mp = _odd_mixer(hp, *od)
            ms = _odd_mixer(hs, *od)
        xp = xp + mod_p[:, 2][:, None, :] * mp
        xs = xs + mod_s[:, 2][:, None, :] * ms
        hp = _modulate(_rmsnorm(xp, norm_ffn[layer]), mod_p[:, 3], mod_p[:, 4])
        hs = _modulate(_rmsnorm(xs, norm_ffn[layer]), mod_s[:, 3], mod_s[:, 4])
        xp = xp + mod_p[:, 5][:, None, :] * _conv_ffn(hp, ffn_in[layer], ffn_conv[layer], ffn_out[layer])
        xs = xs + mod_s[:, 5][:, None, :] * _conv_ffn(hs, ffn_in[layer], ffn_conv[layer], ffn_out[layer])
    new_cache_k = jnp.stack(ks_new, axis=1)
    new_cache_v = jnp.stack(vs_new, axis=1)
    return (xp, xs, new_cache_k, new_cache_v)
```

```python
import math
from contextlib import ExitStack

import numpy as np
import ml_dtypes
import concourse.bass as bass
import concourse.mybir as mybir
from concourse.ap import AP
from concourse.bass_utils import run_bass_kernel_spmd

F32 = mybir.dt.float32
BF16 = mybir.dt.bfloat16
AF = mybir.ActivationFunctionType
ALU = mybir.AluOpType

D = 2048
T = 4608
PAD = 16
SEGS = [(0, 256), (256, 256), (512, 4096)]
TP = T + PAD * (len(SEGS) + 1)
TILES = [(0, 256), (256, 256)] + [(512 + 512 * i, 512) for i in range(8)]
EPS = 1e-6
NCORES = 8


GEO_FULL = dict(T=4608, SEGS=[(0, 256), (256, 256), (512, 4096)], TILES=[(0, 256), (256, 256)] + [(512 + 512 * i, 512) for i in range(8)])
GEO_TAIL = dict(T=1538, SEGS=[(0, 256), (256, 256), (512, 1026)], TILES=[(0, 256), (256, 256), (512, 1), (1537, 1), (513, 512), (1025, 512)])
TP_TAIL = 1538 + 4 * PAD
GEO_C = dict(T=1568, SEGS=[(0, 256), (256, 256), (512, 1056)], TILES=[])
TP_C = 1568 + 4 * PAD


def set_geo(g):
    global T, SEGS, TILES, TP
    T = g["T"]
    SEGS = g["SEGS"]
    TILES = g["TILES"]
    TP = T + PAD * (len(SEGS) + 1)


def ppos(t):
    for si, (s0, ln) in enumerate(SEGS):
        if s0 <= t < s0 + ln:
            return t + PAD * (si + 1)
    raise ValueError(t)


def tile_v(t0):
    return 0 if t0 < 512 else 1


class Buf:
    def __init__(self, t, name):
        self.t = t
        self.name = name
        self.w = {}
        self.r = {}
        self.dsem = None
        self.psum = name.startswith("ps")
        self.pend = 0

    def __getitem__(self, k):
        return self.t[k]


class Eng:
    def __init__(self, obj, sem, idx):
        self.obj = obj
        self.sem = sem
        self.idx = idx
        self.count = 0
        self.seen = {}


class FW:
    def __init__(self, nc, es, ndsem=40):
        self.nc = nc
        self.sems = []
        self.counts = []
        self.E = {}
        for name, obj in (("pe", nc.tensor), ("act", nc.scalar), ("dve", nc.vector), ("pool", nc.gpsimd), ("sp", nc.sync)):
            s = es.enter_context(nc.semaphore("s_" + name))
            self.sems.append(s)
            self.counts.append(0)
            self.E[name] = Eng(obj, s, len(self.sems) - 1)
        self.free_dsems = []
        for i in range(ndsem):
            s = es.enter_context(nc.semaphore("d_%d" % i))
            self.sems.append(s)
            self.counts.append(0)
            self.free_dsems.append(len(self.sems) - 1)
        self.ps = []
        for i in range(8):
            t = es.enter_context(nc.psum_tensor("ps%d" % i, [128, 512], F32))
            self.ps.append(Buf(t, "ps%d" % i))
        self.ps_i = 0
        self.scopes = []
        self.pending = []
        self.max_pending = 6

    def scope(self):
        fw = self

        class _S:
            def __enter__(s):
                s.es = ExitStack()
                s.es.__enter__()
                s.bufs = []
                fw.scopes.append(s)
                return s

            def __exit__(s, *a):
                fw.barrier()
                for b in s.bufs:
                    if b.dsem is not None:
                        fw.free_dsems.append(b.dsem)
                        b.dsem = None
                fw.scopes.pop()
                return s.es.__exit__(*a)
        return _S()

    def sb(self, name, shape, dtype):
        s = self.scopes[-1]
        self.uid = getattr(self, "uid", 0) + 1
        name = "%s_u%d" % (name, self.uid)
        t = s.es.enter_context(self.nc.sbuf_tensor(name, list(shape), dtype))
        b = Buf(t, name)
        s.bufs.append(b)
        return b

    def psum(self):
        b = self.ps[self.ps_i]
        self.ps_i = (self.ps_i + 1) % 8
        return b

    def _wait(self, e, toks):
        E = self.E[e]
        for si, val in toks.items():
            if e == "pe" and si == E.idx:
                continue
            if E.seen.get(si, 0) < val:
                E.obj.wait_ge(self.sems[si], val)
                E.seen[si] = val

    @staticmethod
    def _merge(d, o):
        for k, v in o.items():
            if d.get(k, 0) < v:
                d[k] = v

    def _flush_for(self, bufs):
        while any(b.pend for b in bufs):
            self._emit_store(*self.pending.pop(0))

    def flush_all(self):
        while self.pending:
            self._emit_store(*self.pending.pop(0))

    def op(self, e, fn, r=(), w=()):
        self._flush_for(w)
        E = self.E[e]
        toks = {}
        for b in r:
            self._merge(toks, b.w)
            if b.psum:
                self._merge(toks, b.r)
        for b in w:
            self._merge(toks, b.w)
            self._merge(toks, b.r)
        self._wait(e, toks)
        ins = fn()
        self.counts[E.idx] += 1
        ins.then_inc(E.sem, 1)
        tok = {E.idx: self.counts[E.idx]}
        for b in w:
            b.w = dict(tok)
            b.r = {}
        for b in r:
            if b not in w:
                self._merge(b.r, tok)
        return ins

    def mm(self, psb, out_ap, pairs, reads):
        E = self.E["pe"]
        toks = {}
        for b in reads:
            self._merge(toks, b.w)
        self._merge(toks, psb.w)
        self._merge(toks, psb.r)
        self._wait("pe", toks)
        n = len(pairs)
        for i, (l, rr) in enumerate(pairs):
            ins = self.nc.tensor.matmul(out_ap, l, rr, start=(i == 0), stop=(i == n - 1))
        self.counts[E.idx] += 1
        ins.then_inc(E.sem, 1)
        tok = {E.idx: self.counts[E.idx]}
        psb.w = dict(tok)
        psb.r = {}
        for b in reads:
            self._merge(b.r, tok)

    def transpose(self, psb, out_ap, in_ap, ident_ap, reads):
        E = self.E["pe"]
        toks = {}
        for b in reads:
            self._merge(toks, b.w)
        self._merge(toks, psb.w)
        self._merge(toks, psb.r)
        self._wait("pe", toks)
        ins = self.nc.tensor.transpose(out_ap, in_ap, ident_ap)
        self.counts[E.idx] += 1
        ins.then_inc(E.sem, 1)
        tok = {E.idx: self.counts[E.idx]}
        psb.w = dict(tok)
        psb.r = {}
        for b in reads:
            self._merge(b.r, tok)

    def dma(self, q, out_ap, in_ap, sb, load, **kw):
        if load:
            self._flush_for([sb])
            self._dma_now(q, out_ap, in_ap, sb, True, **kw)
        else:
            sb.pend += 1
            self.pending.append((q, out_ap, in_ap, sb, kw))
            while len(self.pending) > self.max_pending:
                self._emit_store(*self.pending.pop(0))

    def _emit_store(self, q, out_ap, in_ap, sb, kw):
        sb.pend -= 1
        self._dma_now(q, out_ap, in_ap, sb, False, **kw)

    def _dma_now(self, q, out_ap, in_ap, sb, load, **kw):
        E = self.E[q]
        toks = {}
        self._merge(toks, sb.w)
        if load:
            self._merge(toks, sb.r)
        self._wait(q, toks)
        if sb.dsem is None:
            sb.dsem = self.free_dsems.pop(0)
        kw.setdefault("allow_slow_non_contiguous", True)
        ins = E.obj.dma_start(out=out_ap, in_=in_ap, **kw)
        self.counts[sb.dsem] += 16
        ins.then_inc(self.sems[sb.dsem], 16)
        tok = {sb.dsem: self.counts[sb.dsem]}
        if load:
            sb.w = dict(tok)
            sb.r = {}
        else:
            self._merge(sb.r, tok)

    def load(self, q, sb, out_ap, in_ap, **kw):
        self.dma(q, out_ap, in_ap, sb, True, **kw)

    def store(self, q, sb, out_ap, in_ap, **kw):
        self.dma(q, out_ap, in_ap, sb, False, **kw)

    def barrier(self):
        self.flush_all()
        allt = {i: c for i, c in enumerate(self.counts) if c > 0}
        for e in self.E:
            E = self.E[e]
            for si, val in allt.items():
                if E.seen.get(si, 0) < val:
                    E.obj.wait_ge(self.sems[si], val)
                    E.seen[si] = val


def build(upto=99, debug=()):
    set_geo(GEO_FULL)
    nc = bass.Bass("TRN2", target_bir_lowering=False)
    dbg = set(debug)

    def din(name, shape, dt=F32):
        return nc.dram_tensor(name, list(shape), dt, kind="ExternalInput").ap()

    def dout(name, shape, dt=F32):
        return nc.dram_tensor(name, list(shape), dt, kind="ExternalOutput").ap()

    def dscr(name, shape, dt=BF16):
        return nc.dram_tensor(name, list(shape), dt, kind=("ExternalOutput" if name in dbg else "Internal")).ap()

    xT = din("xT", [D, T])
    csil = din("csil", [128, 32])
    ada_w = din("ada_w", [2, D, 6 * D])
    adab = din("adab", [128, 2, 96])
    nmix = din("nmix", [128, 2, 16])
    nffn = din("nffn", [128, 2, 16])
    e_w_in = din("e_w_in", [D, 6144])
    e_w_out = din("e_w_out", [D, D])
    o_w_in = din("o_w_in", [D, 5120])
    o_w_out = din("o_w_out", [D, D])
    ffn_in = din("ffn_in", [2, D, 11264])
    ffn_out = din("ffn_out", [2, 5632, D])
    conv_a = din("conv_a", [128, 8, 3])
    qkn = din("qkn", [128, 2])
    rpbp = din("rpbp", [8, 15, 127])
    ckT = din("ckT", [8, 128, 512])
    cv = din("cv", [512, 1024])
    ffn_conv = din("ffn_conv", [128, 2, 44, 3])
    j64 = din("j64", [64, 64])
    colmask = din("colmask", [128, 64])
    conf_dw = din("conf_dw", [128, 8, 31])
    conf_v = din("conf_v", [128, 3, 8])
    hy_sw = din("hy_sw", [128, 24, 3])
    hy_sb = din("hy_sb", [128, 24])
    hy_w1 = din("hy_w1", [33, 64])
    hy_w2 = din("hy_w2", [64, 64])
    hy_w3 = din("hy_w3", [64, 2048])
    hy_v = din("hy_v", [64, 4])
    hy_bias = din("hy_bias", [128, 8])
    absdel = din("absdel", [128, 1024])
    identf = din("identf", [128, 128])
    HYC = {}
    for L_ in (256, 4096):
        nbk = min(512, L_)
        HYC[L_] = dict(
            zT=din("zT%d" % L_, [33, L_]), negt=din("negt%d" % L_, [128, L_ // 128]),
            fwc=din("fwc%d" % L_, [L_ // 128, 128, L_ // 128, 128], BF16), fws=din("fws%d" % L_, [L_ // 128, 128, L_ // 128, 128], BF16),
            ivc=din("ivc%d" % L_, [L_ // nbk, 128, L_ // 128, nbk], BF16), ivs=din("ivs%d" % L_, [L_ // nbk, 128, L_ // 128, nbk], BF16),
            HS=dscr("HS%d" % L_, [2, L_, 1024], F32))
    qsel = din("qsel", [128, 4])
    ivcq = din("ivcq", [2, 128, 32, 512], BF16)
    ivsq = din("ivsq", [2, 128, 32, 512], BF16)
    ivh = din("ivh", [2, 128, 32, 2], BF16)
    hmask = din("hmask", [128, 2])
    yT = dout("yT", [D, T]) if upto < 8 else None
    yQ = dout("yQ", [D, GEO_TAIL["T"]]) if upto >= 8 else None
    kT_out = dout("kT_out", [1024, 512])
    v_out = dout("v_out", [512, 1024])
    xA = dscr("xA", [D, T], F32)
    xB = dscr("xB", [D, T], F32)
    PJ = dscr("PJ", [11264, TP])
    QN = dscr("QN", [1024, TP])
    KN = dscr("KN", [1024, TP])
    VT = dscr("VT", [T, 1024])
    MI = dscr("MI", [D, TP])
    FF = dscr("FF", [5632, TP])
    CV = dscr("CV", [1024, TP])
    PJ2 = dscr("PJ2", [11264, TP_TAIL])
    MI2 = dscr("MI2", [D, TP_TAIL])
    CV2 = dscr("CV2", [1024, TP_C])
    FF2 = dscr("FF2", [5632, TP_TAIL])
    xQ = dscr("xQ", [D, GEO_TAIL["T"]], F32)
    xA2 = dscr("xA2", [D, GEO_TAIL["T"]], F32)
    UU = dscr("UU", [1024, TP])
    X0 = dscr("X0", [1024, TP])
    YD = dscr("YD", [8, 2, 128, 32, 128])

    es = ExitStack()
    with es:
        fw = FW(nc, es)
        with fw.scope():
            ones_bf = fw.sb("ones_bf", [128, 128], BF16)
            eps_t = fw.sb("eps_t", [128, 1], F32)
            modA1 = [fw.sb("modA1_%d" % l, [128, 16, 2], F32) for l in range(2)]
            modB1 = [fw.sb("modB1_%d" % l, [128, 16, 2], F32) for l in range(2)]
            modG1 = [fw.sb("modG1_%d" % l, [128, 16, 2], F32) for l in range(2)]
            modA2 = [fw.sb("modA2_%d" % l, [128, 16, 2], F32) for l in range(2)]
            modB2 = [fw.sb("modB2_%d" % l, [128, 16, 2], F32) for l in range(2)]
            modG2 = [fw.sb("modG2_%d" % l, [128, 16, 2], F32) for l in range(2)]
            fw.op("dve", lambda: nc.vector.memset(ones_bf[:], 1.0), w=[ones_bf])
            fw.op("dve", lambda: nc.vector.memset(eps_t[:], EPS), w=[eps_t])

            with fw.scope():
                cs = fw.sb("cs", [128, 32], F32)
                csb = fw.sb("csb", [128, 16, 2], BF16)
                adab_t = fw.sb("adab_t", [128, 2, 96], F32)
                nmix_t = fw.sb("nmix_t", [128, 2, 16], F32)
                nffn_t = fw.sb("nffn_t", [128, 2, 16], F32)
                mod = fw.sb("mod", [128, 96, 2], F32)
                tmp = fw.sb("tmpm", [128, 16, 2], F32)
                wts = [fw.sb("adw%d" % i, [128, 16, 512], BF16) for i in range(2)]
                fw.load("sp", cs, cs[:], csil)
                fw.load("sp", adab_t, adab_t[:], adab)
                fw.load("sp", nmix_t, nmix_t[:], nmix)
                fw.load("sp", nffn_t, nffn_t[:], nffn)
                fw.op("act", lambda: nc.scalar.activation(out=csb[:].rearrange("p a b -> p (a b)"), in_=cs[:], func=AF.Silu), r=[cs], w=[csb])
                wi = 0
                for l in range(2):
                    psb = fw.psum()
                    wv = ada_w[l].rearrange("(kc p) n -> p kc n", p=128)
                    for wt in range(24):
                        wb = wts[wi % 2]
                        wi += 1
                        fw.load("pool", wb, wb[:], wv[:, :, wt * 512:(wt + 1) * 512])
                        for c in range(4):
                            fc = wt * 4 + c
                            fw.mm(psb, psb[:, fc * 2:fc * 2 + 2],
                                  [(wb[:, kc, c * 128:(c + 1) * 128], csb[:, kc, :]) for kc in range(16)], [wb, csb])
                    fw.op("dve", lambda: nc.vector.tensor_tensor(out=mod[:], in0=psb[:, 0:192].rearrange("p (a b) -> p a b", b=2),
                                                                 in1=adab_t[:, l, :].unsqueeze(2).to_broadcast([128, 96, 2]), op=ALU.add),
                          r=[psb, adab_t], w=[mod])
                    for (A, Bm, G, nt, base) in ((modA1[l], modB1[l], modG1[l], nmix_t, 0), (modA2[l], modB2[l], modG2[l], nffn_t, 48)):
                        fw.op("dve", lambda: nc.vector.tensor_scalar(out=tmp[:], in0=mod[:, base + 16:base + 32, :], scalar1=1.0, scalar2=None, op0=ALU.add), r=[mod], w=[tmp])
                        fw.op("dve", lambda: nc.vector.tensor_tensor(out=A[:], in0=tmp[:], in1=nt[:, l, :].unsqueeze(2).to_broadcast([128, 16, 2]), op=ALU.mult), r=[tmp, nt], w=[A])
                        fw.op("dve", lambda: nc.vector.tensor_copy(out=Bm[:], in_=mod[:, base:base + 16, :]), r=[mod], w=[Bm])
                        fw.op("dve", lambda: nc.vector.tensor_copy(out=G[:], in_=mod[:, base + 32:base + 48, :]), r=[mod], w=[G])

            def norm_phase(x_src, A, Bm, hT):
                with fw.scope():
                    xts = [fw.sb("nx%d" % i, [128, 16, 256], F32) for i in range(2)]
                    sq = fw.sb("nsq", [128, 16, 256], BF16)
                    rs = fw.sb("nrs", [128, 256], F32)
                    xv = x_src.rearrange("(kc p) t -> p kc t", p=128)
                    subt = [(t0_ + s_, min(256, n_ - s_)) for (t0_, n_) in TILES for s_ in range(0, n_, 256)]
                    for i, (t0, nn) in enumerate(subt):
                        v = tile_v(t0)
                        xt = xts[i % 2]
                        fw.load("sp" if i % 2 == 0 else "pool", xt, xt[:, :, 0:nn], xv[:, :, t0:t0 + nn])
                        fw.op("act", lambda: nc.scalar.activation(out=sq[:, :, 0:nn], in_=xt[:, :, 0:nn], func=AF.Square), r=[xt], w=[sq])
                        psb = fw.psum()
                        fw.mm(psb, psb[:, 0:nn], [(ones_bf[:], sq[:, kc, 0:nn]) for kc in range(16)], [ones_bf, sq])
                        fw.op("act", lambda: nc.scalar.activation(out=rs[:, 0:nn], in_=psb[:, 0:nn], func=AF.Sqrt, bias=eps_t[:, 0:1], scale=1.0 / D), r=[psb, eps_t], w=[rs])
                        fw.op("dve", lambda: nc.vector.reciprocal(out=rs[:, 0:nn], in_=rs[:, 0:nn]), r=[rs], w=[rs])
                        fw.op("dve", lambda: nc.vector.tensor_tensor(out=xt[:, :, 0:nn], in0=xt[:, :, 0:nn], in1=rs[:, 0:nn].unsqueeze(1).to_broadcast([128, 16, nn]), op=ALU.mult), r=[xt, rs], w=[xt])
                        fw.op("dve", lambda: nc.vector.tensor_tensor(out=xt[:, :, 0:nn], in0=xt[:, :, 0:nn], in1=A[:, :, v].unsqueeze(2).to_broadcast([128, 16, nn]), op=ALU.mult), r=[xt, A], w=[xt])
                        fw.op("pool", lambda: nc.gpsimd.tensor_tensor(out=hT[:, :, t0:t0 + nn], in0=xt[:, :, 0:nn], in1=Bm[:, :, v].unsqueeze(2).to_broadcast([128, 16, nn]), op=ALU.add), r=[xt, Bm], w=[hT])

            evac_i = [0]

            def evac(out_ap, in_ap, r, w, e=None):
                evac_i[0] += 1
                if (evac_i[0] % 2) if e is None else (e % 2):
                    fw.op("act", lambda: nc.scalar.copy(out=out_ap, in_=in_ap), r=r, w=w)
                else:
                    fw.op("dve", lambda: nc.vector.tensor_copy(out=out_ap, in_=in_ap), r=r, w=w)

            def proj_rows(actT, KC, W, col0, ncols, dst, dst_row0, wts, rows):
                wv = W.rearrange("(kc p) n -> p kc n", p=128)
                nw = ncols // 512
                fw.load("pool", wts[0], wts[0][:, 0:KC, :], wv[:, :, col0:col0 + 512])
                ri = 0
                for wt in range(nw):
                    wb = wts[wt % 2]
                    if wt + 1 < nw:
                        nb = wts[(wt + 1) % 2]
                        fw.load("pool", nb, nb[:, 0:KC, :], wv[:, :, col0 + (wt + 1) * 512:col0 + (wt + 2) * 512])
                    for c in range(4):
                        row = rows[ri % len(rows)]
                        ri += 1
                        for (t0, n) in TILES:
                            psb = fw.psum()
                            fw.mm(psb, psb[:, 0:n], [(wb[:, kc, c * 128:(c + 1) * 128], actT[:, kc, t0:t0 + n]) for kc in range(KC)], [wb, actT])
                            p0 = ppos(t0)
                            evac(row[:, p0:p0 + n], psb[:, 0:n], [psb], [row], ri)
                        r0 = dst_row0 + (wt * 4 + c) * 128
                        fw.store("sp", row, dst[r0:r0 + 128, :], row[:])

            def make_rows(n=3):
                rows = [fw.sb("prow%d" % i, [128, TP], BF16) for i in range(n)]
                for rr in rows:
                    fw.op("pool", lambda: nc.gpsimd.memset(rr[:], 0.0), w=[rr])
                return rows

            def proj_resid(actT, KC, W, x_src, x_dst, G, wts):
                wv = W.rearrange("(kc p) n -> p kc n", p=128)
                xts = [fw.sb("rx%d" % i, [128, 512], F32) for i in range(3)]
                xos = [fw.sb("ro%d" % i, [128, 512], F32) for i in range(3)]
                nw = D // 512
                fw.load("pool", wts[0], wts[0][:, 0:KC, :], wv[:, :, 0:512])
                k = 0
                for wt in range(nw):
                    wb = wts[wt % 2]
                    if wt + 1 < nw:
                        nb = wts[(wt + 1) % 2]
                        fw.load("pool", nb, nb[:, 0:KC, :], wv[:, :, (wt + 1) * 512:(wt + 2) * 512])
                    for c in range(4):
                        oc = wt * 4 + c
                        for (t0, n) in TILES:
                            v = tile_v(t0)
                            xt = xts[k % 3]
                            xo = xos[k % 3]
                            k += 1
                            fw.load("sp", xt, xt[:, 0:n], x_src[oc * 128:(oc + 1) * 128, t0:t0 + n])
                            psb = fw.psum()
                            fw.mm(psb, psb[:, 0:n], [(wb[:, kc, c * 128:(c + 1) * 128], actT[:, kc, t0:t0 + n]) for kc in range(KC)], [wb, actT])
                            fw.op("dve", lambda: nc.vector.scalar_tensor_tensor(out=xo[:, 0:n], in0=psb[:, 0:n], scalar=G[:, oc, v:v + 1], in1=xt[:, 0:n], op0=ALU.mult, op1=ALU.add),
                                  r=[psb, G, xt], w=[xo])
                            fw.store("sp", xo, x_dst[oc * 128:(oc + 1) * 128, t0:t0 + n], xo[:, 0:n])

            if upto >= 1:
                with fw.scope():
                    hT = fw.sb("hT", [128, 16, T], BF16)
                    norm_phase(xT, modA1[0], modB1[0], hT)
                    with fw.scope():
                      if upto >= 1.3:
                        wts = [fw.sb("wt%d" % i, [128, 16, 512], BF16) for i in range(2)]
                        rows = make_rows(2)
                        proj_rows(hT, 16, e_w_in, 0, 5120, PJ, 0, wts, rows)
                    with fw.scope():
                      if upto >= 1.6:
                        wts = [fw.sb("wt%d" % i, [128, 16, 512], BF16) for i in range(2)]
                        wv = e_w_in.rearrange("(kc p) n -> p kc n", p=128)
                        vst = [fw.sb("vst%d" % i, [128, 512], BF16) for i in range(2)]
                        vst32 = [fw.sb("vstf%d" % i, [128, 512], F32) for i in range(2)]
                        k = 0
                        for half in range(2):
                            wb = wts[half]
                            fw.load("pool", wb, wb[:], wv[:, :, 5120 + half * 512:5120 + (half + 1) * 512])
                            for tb in range(T // 128):
                                psb = fw.psum()
                                fw.mm(psb, psb[:, :], [(hT[:, kc, tb * 128:(tb + 1) * 128], wb[:, kc, :]) for kc in range(16)], [wb, hT])
                                vs = vst[k % 2]
                                k += 1
                                if tb < 4:
                                    vf = vst32[k % 2]
                                    evac(vf[:], psb[:, :], [psb], [vf])
                                    fw.store("sp", vf, v_out[tb * 128:(tb + 1) * 128, half * 512:(half + 1) * 512], vf[:])
                                    fw.op("pool", lambda: nc.gpsimd.tensor_copy(out=vs[:], in_=vf[:]), r=[vf], w=[vs])
                                else:
                                    evac(vs[:], psb[:, :], [psb], [vs])
                                fw.store("sp", vs, VT[tb * 128:(tb + 1) * 128, half * 512:(half + 1) * 512], vs[:])

            if upto >= 2.0:
                with fw.scope():
                    cw = fw.sb("cw", [128, 8, 3], F32)
                    qk_g = fw.sb("qk_g", [128, 2], F32)
                    fw.load("sp", cw, cw[:], conv_a)
                    fw.load("sp", qk_g, qk_g[:], qkn)
                    rb = [[fw.sb("scr%d_%d" % (i, j), [128, TP], BF16) for j in range(3)] for i in range(2)]
                    pp = fw.sb("scp", [128, TP], F32)
                    oo = fw.sb("sco", [128, TP], F32)
                    ob = [fw.sb("scob%d" % i, [128, TP], BF16) for i in range(2)]
                    for j in range(8):
                        ab, ac, ax = rb[j % 2]
                        fw.load("sp", ab, ab[:], PJ[j * 128:(j + 1) * 128, :])
                        fw.load("sp", ac, ac[:], PJ[(8 + j) * 128:(9 + j) * 128, :])
                        fw.load("pool", ax, ax[:], PJ[(16 + j) * 128:(17 + j) * 128, :])
                        fw.op("dve", lambda: nc.vector.tensor_tensor(out=pp[:], in0=ac[:], in1=ax[:], op=ALU.mult), r=[ac, ax], w=[pp])
                        fw.op("dve", lambda: nc.vector.tensor_scalar(out=oo[:, 1:TP - 1], in0=pp[:, 0:TP - 2], scalar1=cw[:, j, 0:1], scalar2=None, op0=ALU.mult), r=[pp, cw], w=[oo])
                        fw.op("dve", lambda: nc.vector.scalar_tensor_tensor(out=oo[:, 1:TP - 1], in0=pp[:, 1:TP - 1], scalar=cw[:, j, 1:2], in1=oo[:, 1:TP - 1], op0=ALU.mult, op1=ALU.add), r=[pp, cw, oo], w=[oo])
                        fw.op("dve", lambda: nc.vector.scalar_tensor_tensor(out=oo[:, 1:TP - 1], in0=pp[:, 2:TP], scalar=cw[:, j, 2:3], in1=oo[:, 1:TP - 1], op0=ALU.mult, op1=ALU.add), r=[pp, cw, oo], w=[oo])
                        o = ob[j % 2]
                        fw.op("dve", lambda: nc.vector.tensor_tensor(out=o[:, 1:TP - 1], in0=oo[:, 1:TP - 1], in1=ab[:, 1:TP - 1], op=ALU.mult), r=[oo, ab], w=[o])
                        fw.store("sp", o, MI[j * 128:(j + 1) * 128, 1:TP - 1], o[:, 1:TP - 1])
                    sq = fw.sb("qsq", [128, TP], BF16)
                    rs = fw.sb("qrs", [128, TP], F32)
                    kf = fw.sb("kf32", [128, 544], F32)
                    i = 0
                    for which in range(2):
                        for h in range(8):
                            row = rb[i % 2][0]
                            o = ob[i % 2]
                            i += 1
                            fw.load("sp", row, row[:], PJ[(24 + which * 8 + h) * 128:(25 + which * 8 + h) * 128, :])
                            fw.op("act", lambda: nc.scalar.activation(out=sq[:], in_=row[:], func=AF.Square), r=[row], w=[sq])
                            for c0 in range(0, TP, 512):
                                n = min(512, TP - c0)
                                psb = fw.psum()
                                fw.mm(psb, psb[:, 0:n], [(ones_bf[:], sq[:, c0:c0 + n])], [ones_bf, sq])
                                fw.op("act", lambda: nc.scalar.activation(out=rs[:, c0:c0 + n], in_=psb[:, 0:n], func=AF.Sqrt, bias=eps_t[:, 0:1], scale=1.0 / 128), r=[psb, eps_t], w=[rs])
                            fw.op("dve", lambda: nc.vector.reciprocal(out=rs[:], in_=rs[:]), r=[rs], w=[rs])
                            fw.op("dve", lambda: nc.vector.scalar_tensor_tensor(out=o[:], in0=row[:], scalar=qk_g[:, which:which + 1], in1=rs[:], op0=ALU.mult, op1=ALU.mult), r=[row, qk_g, rs], w=[o])
                            dst = QN if which == 0 else KN
                            fw.store("sp", o, dst[h * 128:(h + 1) * 128, :], o[:])
                            if which == 1:
                                fw.op("dve", lambda: nc.vector.scalar_tensor_tensor(out=kf[:], in0=row[:, 0:544], scalar=qk_g[:, 1:2], in1=rs[:, 0:544], op0=ALU.mult, op1=ALU.mult), r=[row, qk_g, rs], w=[kf])
                                fw.store("sp", kf, kT_out[h * 128:(h + 1) * 128, 0:256], kf[:, 16:272])
                                fw.store("sp", kf, kT_out[h * 128:(h + 1) * 128, 256:512], kf[:, 288:544])


            SP0 = ppos(512)
            SCALE = 128.0 ** -0.5

            def attention_phase():
                with fw.scope():
                    vall = fw.sb("vall", [128, 36, 1024], BF16)
                    ck = fw.sb("ck", [128, 8, 512], BF16)
                    cvb = fw.sb("cvb", [128, 4, 1024], BF16)
                    fw.load("sp", vall, vall[:], VT.rearrange("(b p) c -> p b c", p=128))
                    fw.load("pool", ck, ck[:], ckT.rearrange("h d k -> d h k"))
                    fw.load("pool", cvb, cvb[:], cv.rearrange("(b p) c -> p b c", p=128))
                    MG = fw.sb("MG", [128, 8, 5, 128], BF16)
                    ME = [fw.sb("ME%d" % i, [128, 8, 4, 128], BF16) for i in range(4)]
                    with fw.scope():
                        tz = fw.sb("tz", [128, 8, 15, 64], BF16)
                        jt = fw.sb("jt", [64, 64], F32)
                        cm = fw.sb("cm", [128, 64], F32)
                        fw.load("sp", jt, jt[:], j64)
                        fw.load("sp", cm, cm[:], colmask)
                        hh = [fw.sb("hh%d" % i, [64, 15, 128], F32) for i in range(2)]
                        te = fw.sb("te", [128, 15, 64], F32)
                        for h in range(8):
                            hb = hh[h % 2]
                            src = AP(rpbp.tensor, h * 15 * 127, [[1, 64], [127, 15], [1, 64]])
                            fw.load("sp", hb, hb[:, :, 0:64], src)
                            fw.load("sp", hb, hb[:, :, 64:128], src)
                            p1 = fw.psum()
                            p2 = fw.psum()
                            for a in range(15):
                                pb = p1 if a < 8 else p2
                                fw.mm(pb, pb[:, (a % 8) * 64:(a % 8) * 64 + 64], [(hb[:, a, :], jt[:])], [hb, jt])
                            fw.op("act", lambda: nc.scalar.activation(out=te[:, 0:8, :], in_=p1[:, :].rearrange("p (a k) -> p a k", k=64), func=AF.Exp), r=[p1], w=[te])
                            fw.op("act", lambda: nc.scalar.activation(out=te[:, 8:15, :], in_=p2[:, 0:448].rearrange("p (a k) -> p a k", k=64), func=AF.Exp), r=[p2], w=[te])
                            fw.op("dve", lambda: nc.vector.tensor_tensor(out=tz[:, h, :, :], in0=te[:], in1=cm[:].unsqueeze(1).to_broadcast([128, 15, 64]), op=ALU.mult), r=[te, cm], w=[tz])
                        for M in [MG] + ME:
                            fw.op("pool", lambda: nc.gpsimd.memset(M[:], 0.0), w=[M])
                        defs = [(MG, 5, -4, True)] + [(ME[0], 4, 0, False), (ME[1], 4, -2, False), (ME[2], 4, -4, False), (ME[3], 4, -6, False)]
                        for (M, nb, off, chk) in defs:
                            for b in range(nb):
                                for kr in range(2):
                                    for qr in range(2):
                                        dr = 2 * b + kr + off - qr
                                        if chk and not (-4 <= dr <= 3):
                                            continue
                                        a = dr + 7
                                        assert 0 <= a <= 14
                                        fw.op("dve", lambda: nc.vector.tensor_copy(out=M[kr * 64:(kr + 1) * 64, :, b, qr * 64:(qr + 1) * 64], in_=tz[kr * 64:(kr + 1) * 64, :, a, :]), r=[tz], w=[M])
                    qrows = [fw.sb("aq%d" % i, [128, TP], BF16) for i in range(2)]
                    krows = [fw.sb("ak%d" % i, [128, TP], BF16) for i in range(2)]
                    yrows = [fw.sb("ay%d" % i, [128, TP], BF16) for i in range(2)]
                    for yr in yrows:
                        fw.op("pool", lambda: nc.gpsimd.memset(yr[:], 0.0), w=[yr])
                    els = [fw.sb("el%d" % i, [128, 640], F32) for i in range(2)]
                    ets = [fw.sb("et%d" % i, [128, 640], BF16) for i in range(2)]
                    ecs = [fw.sb("ec%d" % i, [128, 512], BF16) for i in range(2)]
                    rcs = [fw.sb("rc%d" % i, [128, 256], F32) for i in range(2)]
                    it = 0
                    import os
                    ATT = int(os.environ.get("ATT_STAGE", "9"))
                    for h in range(8 if ATT >= 1 else 0):
                        qh = qrows[h % 2]
                        kh = krows[h % 2]
                        yr = yrows[h % 2]
                        fw.load("sp", qh, qh[:], QN[h * 128:(h + 1) * 128, :])
                        fw.load("pool", kh, kh[:], KN[h * 128:(h + 1) * 128, :])
                        hs = slice(h * 128, (h + 1) * 128)
                        for s_ in range(2):
                            p0 = ppos(s_ * 256)
                            et = ets[it % 2]
                            rc = rcs[it % 2]
                            it += 1
                            pS = fw.psum()
                            for kb in range(2):
                                fw.mm(pS, pS[:, kb * 256:(kb + 1) * 256], [(kh[:, p0 + kb * 128:p0 + (kb + 1) * 128], qh[:, p0:p0 + 256])], [kh, qh])
                            fw.op("act", lambda: nc.scalar.activation(out=et[:, 0:512], in_=pS[:, :], func=AF.Exp, scale=SCALE), r=[pS], w=[et])
                            pO = fw.psum()
                            fw.mm(pO, pO[:, 0:256], [(vall[:, s_ * 2 + kb, hs], et[:, kb * 256:(kb + 1) * 256]) for kb in range(2)], [vall, et])
                            fw.mm(pO, pO[:, 256:512], [(ones_bf[:], et[:, kb * 256:(kb + 1) * 256]) for kb in range(2)], [ones_bf, et])
                            fw.op("dve", lambda: nc.vector.reciprocal(out=rc[:, 0:256], in_=pO[:, 256:512]), r=[pO], w=[rc])
                            fw.op("dve", lambda: nc.vector.tensor_tensor(out=yr[:, p0:p0 + 256], in0=pO[:, 0:256], in1=rc[:, 0:256], op=ALU.mult), r=[pO, rc], w=[yr])
                        for m in ([int(x) for x in os.environ['ATT_M'].split(',')] if 'ATT_M' in os.environ else range(32 if ATT >= 2 else 0)):
                            if m < 2:
                                lb = [0, 1, 2, 3]
                                M = ME[m]
                            elif m >= 30:
                                lb = [28, 29, 30, 31]
                                M = ME[2 + m - 30]
                            else:
                                lb = [m - 2, m - 1, m, m + 1, m + 2]
                                M = MG
                            nb = len(lb)
                            el = els[it % 2]
                            et = ets[it % 2]
                            ec = ecs[it % 2]
                            rc = rcs[it % 2]
                            it += 1
                            q0 = SP0 + m * 128
                            qt = qh[:, q0:q0 + 128]
                            pL = fw.psum()
                            pC = fw.psum()
                            pO = fw.psum()
                            for i in range(4):
                                fw.mm(pL, pL[:, i * 128:(i + 1) * 128], [(kh[:, SP0 + lb[i] * 128:SP0 + (lb[i] + 1) * 128], qt)], [kh, qh])
                            if nb == 5:
                                fw.mm(pO, pO[:, 256:384], [(kh[:, SP0 + lb[4] * 128:SP0 + (lb[4] + 1) * 128], qt)], [kh, qh])
                            for cb in range(4):
                                fw.mm(pC, pC[:, cb * 128:(cb + 1) * 128], [(ck[:, h, cb * 128:(cb + 1) * 128], qt)], [ck, qh])
                            fw.op("act", lambda: nc.scalar.activation(out=el[:, 0:512], in_=pL[:, :], func=AF.Exp, scale=SCALE), r=[pL], w=[el])
                            if nb == 5:
                                fw.op("act", lambda: nc.scalar.activation(out=el[:, 512:640], in_=pO[:, 256:384], func=AF.Exp, scale=SCALE), r=[pO], w=[el])
                            fw.op("act", lambda: nc.scalar.activation(out=ec[:], in_=pC[:, :], func=AF.Exp, scale=SCALE), r=[pC], w=[ec])
                            fw.op("dve", lambda: nc.vector.tensor_tensor(out=et[:, 0:nb * 128], in0=el[:, 0:nb * 128], in1=M[:, h, :, :].rearrange("p b q -> p (b q)"), op=ALU.mult), r=[el, M], w=[et])
                            pairsO = [(vall[:, 4 + lb[i], hs], et[:, i * 128:(i + 1) * 128]) for i in range(nb)] + [(cvb[:, cb, hs], ec[:, cb * 128:(cb + 1) * 128]) for cb in range(4)]
                            pairsS = [(ones_bf[:], et[:, i * 128:(i + 1) * 128]) for i in range(nb)] + [(ones_bf[:], ec[:, cb * 128:(cb + 1) * 128]) for cb in range(4)]
                            fw.mm(pO, pO[:, 0:128], pairsO, [vall, cvb, et, ec])
                            fw.mm(pO, pO[:, 128:256], pairsS, [ones_bf, et, ec])
                            fw.op("dve", lambda: nc.vector.reciprocal(out=rc[:, 0:128], in_=pO[:, 128:256]), r=[pO], w=[rc])
                            fw.op("dve", lambda: nc.vector.tensor_tensor(out=yr[:, q0:q0 + 128], in0=pO[:, 0:128], in1=rc[:, 0:128], op=ALU.mult), r=[pO, rc], w=[yr])
                        fw.store("sp", yr, MI[1024 + h * 128:1024 + (h + 1) * 128, :], yr[:])

            def load_act_from(src, actT):
                sv = src.rearrange("(kc p) t -> p kc t", p=128)
                for (s0, ln) in SEGS:
                    p0 = ppos(s0)
                    fw.load("sp", actT, actT[:, :, s0:s0 + ln], sv[:, :, p0:p0 + ln])

            def wout_phase(W, x_src, x_dst, G):
                with fw.scope():
                    mT = fw.sb("mT", [128, 16, T], BF16)
                    load_act_from(MI, mT)
                    wts = [fw.sb("wo%d" % i, [128, 16, 512], BF16) for i in range(2)]
                    proj_resid(mT, 16, W, x_src, x_dst, G, wts)

            def ffn_phase(l, x_src, x_dst, PJ=PJ, FF=FF, halo=None):
                with fw.scope():
                    hT = fw.sb("hTf", [128, 16, T], BF16)
                    norm_phase(x_src, modA2[l], modB2[l], hT)
                    with fw.scope():
                        wts = [fw.sb("wf%d" % i, [128, 16, 512], BF16) for i in range(2)]
                        rows = make_rows(2)
                        proj_rows(hT, 16, ffn_in[l], 0, 11264, PJ, 0, wts, rows)
                wq_scope = fw.scope()
                wq_scope.__enter__()
                whc = [fw.sb("whc%d" % i, [128, 44, 128], BF16) for i in range(8)]
                wv = ffn_out[l].rearrange("(kc p) n -> p kc n", p=128)

                def load_wc(half, c):
                    col = (half * 8 + c) * 128
                    for h2 in range(2):
                        fw.load("pool", whc[c], whc[c][:, h2 * 22:(h2 + 1) * 22, :], wv[:, h2 * 22:(h2 + 1) * 22, col:col + 128])
                for c_ in range(8):
                    load_wc(0, c_)
                with fw.scope():
                    fcw = fw.sb("fcw", [128, 44, 3], F32)
                    fw.load("sp", fcw, fcw[:], ffn_conv[:, l, :, :])
                    ar = [fw.sb("fa%d" % i, [128, TP], BF16) for i in range(2)]
                    gr = [fw.sb("fg%d" % i, [128, TP], BF16) for i in range(2)]
                    oo = [fw.sb("fo%d" % i, [128, TP], F32) for i in range(2)]
                    ge = [fw.sb("fe%d" % i, [128, TP], BF16) for i in range(2)]
                    ob = [fw.sb("fb%d" % i, [128, TP], BF16) for i in range(2)]
                    for o in ob:
                        fw.op("pool", lambda: nc.gpsimd.memset(o[:], 0.0), w=[o])
                    for j in range(44):
                        a = ar[j % 2]
                        g = gr[j % 2]
                        o32 = oo[j % 2]
                        gg = ge[j % 2]
                        o = ob[j % 2]
                        fw.load("sp", a, a[:], PJ[j * 128:(j + 1) * 128, :])
                        fw.load("pool", g, g[:], PJ[5632 + j * 128:5632 + (j + 1) * 128, :])
                        if halo is not None:
                            for hi_, hp_ in enumerate((ppos(512), ppos(1537))):
                                fw.op("dve", lambda: nc.vector.tensor_scalar(out=a[:, hp_:hp_ + 1], in0=a[:, hp_:hp_ + 1], scalar1=halo[:, hi_:hi_ + 1], scalar2=None, op0=ALU.mult), r=[a, halo], w=[a])
                        fw.op("dve", lambda: nc.vector.tensor_scalar(out=o32[:, 1:TP - 1], in0=a[:, 0:TP - 2], scalar1=fcw[:, j, 0:1], scalar2=None, op0=ALU.mult), r=[a, fcw], w=[o32])
                        fw.op("dve", lambda: nc.vector.scalar_tensor_tensor(out=o32[:, 1:TP - 1], in0=a[:, 1:TP - 1], scalar=fcw[:, j, 1:2], in1=o32[:, 1:TP - 1], op0=ALU.mult, op1=ALU.add), r=[a, fcw, o32], w=[o32])
                        fw.op("dve", lambda: nc.vector.scalar_tensor_tensor(out=o32[:, 1:TP - 1], in0=a[:, 2:TP], scalar=fcw[:, j, 2:3], in1=o32[:, 1:TP - 1], op0=ALU.mult, op1=ALU.add), r=[a, fcw, o32], w=[o32])
                        fw.op("act", lambda: nc.scalar.activation(out=gg[:, 1:TP - 1], in_=o32[:, 1:TP - 1], func=AF.Gelu_apprx_tanh), r=[o32], w=[gg])
                        fw.op("pool", lambda: nc.gpsimd.tensor_tensor(out=o[:, 1:TP - 1], in0=gg[:, 1:TP - 1], in1=g[:, 1:TP - 1], op=ALU.mult), r=[gg, g], w=[o])
                        fw.store("sp", o, FF[j * 128:(j + 1) * 128, :], o[:])
                with fw.scope():
                    G = modG2[l]
                    fts = [fw.sb("ft%d" % i, [128, 44, 512], BF16) for i in range(2)]
                    xts = [fw.sb("fx%d" % i, [128, 512], F32) for i in range(3)]
                    xos = [fw.sb("fxo%d" % i, [128, 512], F32) for i in range(3)]
                    fv = FF.rearrange("(kc p) t -> p kc t", p=128)
                    k = 0
                    ti = 0
                    seq_ = [(half, tix, t0, n) for half in range(2) for tix, (t0, n) in enumerate(TILES)]

                    def load_ft(i_):
                        _, _, t0_, n_ = seq_[i_]
                        fw.load("sp", fts[i_ % 2], fts[i_ % 2][:, :, 0:n_], fv[:, :, ppos(t0_):ppos(t0_) + n_])
                    load_ft(0)
                    for si_, (half, tix, t0, n) in enumerate(seq_):
                        if True:
                            v = tile_v(t0)
                            ft = fts[si_ % 2]
                            if si_ + 1 < len(seq_):
                                load_ft(si_ + 1)
                            for c in range(8):
                                oc = half * 8 + c
                                wh = whc[c]
                                xt = xts[k % 3]
                                xo = xos[k % 3]
                                k += 1
                                fw.load("sp", xt, xt[:, 0:n], x_src[oc * 128:(oc + 1) * 128, t0:t0 + n])
                                psb = fw.psum()
                                fw.mm(psb, psb[:, 0:n], [(wh[:, kc, :], ft[:, kc, 0:n]) for kc in range(44)], [wh, ft])
                                fw.op("dve", lambda: nc.vector.scalar_tensor_tensor(out=xo[:, 0:n], in0=psb[:, 0:n], scalar=G[:, oc, v:v + 1], in1=xt[:, 0:n], op0=ALU.mult, op1=ALU.add),
                                      r=[psb, G, xt], w=[xo])
                                fw.store("sp", xo, x_dst[oc * 128:(oc + 1) * 128, t0:t0 + n], xo[:, 0:n])
                                if half == 0 and tix == len(TILES) - 1:
                                    load_wc(1, c)
                wq_scope.__exit__(None, None, None)

            def conformer_phase():
                with fw.scope():
                    cdw = fw.sb("cdw", [128, 8, 31], F32)
                    cvv = fw.sb("cvv", [128, 3, 8], F32)
                    fw.load("sp", cdw, cdw[:], conf_dw)
                    fw.load("sp", cvv, cvv[:], conf_v)
                    qs = fw.sb("cqs", [128, 4], F32)
                    fw.load("sp", qs, qs[:], qsel)
                    carf = [fw.sb("carf%d" % i, [128, TP], BF16) for i in range(2)]
                    cgrf = [fw.sb("cgrf%d" % i, [128, TP], BF16) for i in range(2)]
                    car = [fw.sb("car%d" % i, [128, TP_C], BF16) for i in range(2)]
                    cgr = [fw.sb("cgr%d" % i, [128, TP_C], BF16) for i in range(2)]
                    sg = fw.sb("csg", [128, TP_C], F32)
                    u32 = fw.sb("cu32", [128, TP_C], F32)
                    oA = fw.sb("coA", [128, TP_C], F32)
                    cvo = [fw.sb("cvo%d" % i, [128, TP_C], BF16) for i in range(2)]
                    for o in cvo + car + cgr:
                        fw.op("pool", lambda: nc.gpsimd.memset(o[:], 0.0), w=[o])
                    W = TP_C - 30
                    for j in range(8):
                        caf = carf[j % 2]
                        cgf = cgrf[j % 2]
                        ca = car[j % 2]
                        cg = cgr[j % 2]
                        o = cvo[j % 2]
                        fw.load("sp", caf, caf[:], PJ[j * 128:(j + 1) * 128, :])
                        fw.load("pool", cgf, cgf[:], PJ[(8 + j) * 128:(9 + j) * 128, :])
                        for (full, qq) in ((caf, ca), (cgf, cg)):
                            fw.op("pool", lambda: nc.gpsimd.tensor_copy(out=qq[:, 0:560], in_=full[:, 0:560]), r=[full], w=[qq])
                            fw.op("dve", lambda: nc.vector.tensor_scalar(out=qq[:, 560:1616], in0=full[:, 544:544 + 1056], scalar1=qs[:, 0:1], scalar2=None, op0=ALU.mult), r=[full, qs], w=[qq])
                            for q in range(1, 4):
                                fw.op("dve", lambda: nc.vector.scalar_tensor_tensor(out=qq[:, 560:1616], in0=full[:, 544 + q * 1024:544 + q * 1024 + 1056], scalar=qs[:, q:q + 1], in1=qq[:, 560:1616], op0=ALU.mult, op1=ALU.add), r=[full, qs, qq], w=[qq])
                        fw.op("act", lambda: nc.scalar.activation(out=sg[:], in_=cg[:], func=AF.Sigmoid), r=[cg], w=[sg])
                        fw.op("dve", lambda: nc.vector.tensor_tensor(out=u32[:], in0=ca[:], in1=sg[:], op=ALU.mult), r=[ca, sg], w=[u32])
                        fw.op("dve", lambda: nc.vector.tensor_scalar(out=oA[:, 15:15 + W], in0=u32[:, 0:W], scalar1=cdw[:, j, 0:1], scalar2=cvv[:, 0, j:j + 1], op0=ALU.mult, op1=ALU.add), r=[u32, cdw, cvv], w=[oA])
                        for k in range(1, 31):
                            fw.op("dve", lambda: nc.vector.scalar_tensor_tensor(out=oA[:, 15:15 + W], in0=u32[:, k:k + W], scalar=cdw[:, j, k:k + 1], in1=oA[:, 15:15 + W], op0=ALU.mult, op1=ALU.add), r=[u32, cdw, oA], w=[oA])
                        fw.op("act", lambda: nc.scalar.copy(out=o[:, 15:15 + W], in_=oA[:, 15:15 + W]), r=[oA], w=[o])
                        fw.store("sp", o, CV2[j * 128:(j + 1) * 128, :], o[:])
                with fw.scope():
                    cvv = fw.sb("cvv2", [128, 3, 8], F32)
                    fw.load("sp", cvv, cvv[:], conf_v)
                    cvts = [fw.sb("cvt%d" % i, [128, 8, 512], BF16) for i in range(2)]
                    sq = fw.sb("lsq", [128, 8, 512], BF16)
                    mean = fw.sb("lmean", [128, 512], F32)
                    msq = fw.sb("lmsq", [128, 512], F32)
                    var = fw.sb("lvar", [128, 512], F32)
                    xc = fw.sb("lxc", [128, 8, 512], F32)
                    mos = [fw.sb("lmo%d" % i, [128, 8, 512], BF16) for i in range(2)]
                    cvw = CV2.rearrange("(j p) t -> p j t", p=128)
                    miw = MI2.rearrange("(j p) t -> p j t", p=128)
                    for ti, (p0s, p0, n) in enumerate([(16, 16, 256), (288, 288, 256), (575, 560, 512), (1087, 1072, 512), (1599, 1584, 2)]):
                        cvt = cvts[ti % 2]
                        mo = mos[ti % 2]
                        fw.load("sp", cvt, cvt[:, :, 0:n], cvw[:, :, p0s:p0s + n])
                        fw.op("act", lambda: nc.scalar.activation(out=sq[:, :, 0:n], in_=cvt[:, :, 0:n], func=AF.Square), r=[cvt], w=[sq])
                        ps1 = fw.psum()
                        ps2 = fw.psum()
                        fw.mm(ps1, ps1[:, 0:n], [(ones_bf[:], cvt[:, j, 0:n]) for j in range(8)], [ones_bf, cvt])
                        fw.mm(ps2, ps2[:, 0:n], [(ones_bf[:], sq[:, j, 0:n]) for j in range(8)], [ones_bf, sq])
                        fw.op("act", lambda: nc.scalar.mul(out=mean[:, 0:n], in_=ps1[:, 0:n], mul=1.0 / 1024), r=[ps1], w=[mean])
                        fw.op("dve", lambda: nc.vector.tensor_tensor(out=msq[:, 0:n], in0=mean[:, 0:n], in1=mean[:, 0:n], op=ALU.mult), r=[mean], w=[msq])
                        fw.op("dve", lambda: nc.vector.scalar_tensor_tensor(out=var[:, 0:n], in0=ps2[:, 0:n], scalar=1.0 / 1024, in1=msq[:, 0:n], op0=ALU.mult, op1=ALU.subtract), r=[ps2, msq], w=[var])
                        fw.op("act", lambda: nc.scalar.activation(out=var[:, 0:n], in_=var[:, 0:n], func=AF.Sqrt, bias=eps_t[:, 0:1], scale=1.0), r=[var, eps_t], w=[var])
                        fw.op("dve", lambda: nc.vector.reciprocal(out=var[:, 0:n], in_=var[:, 0:n]), r=[var], w=[var])
                        fw.op("dve", lambda: nc.vector.tensor_tensor(out=xc[:, :, 0:n], in0=cvt[:, :, 0:n], in1=mean[:, 0:n].unsqueeze(1).to_broadcast([128, 8, n]), op=ALU.subtract), r=[cvt, mean], w=[xc])
                        fw.op("dve", lambda: nc.vector.tensor_tensor(out=xc[:, :, 0:n], in0=xc[:, :, 0:n], in1=var[:, 0:n].unsqueeze(1).to_broadcast([128, 8, n]), op=ALU.mult), r=[xc, var], w=[xc])
                        for j in range(8):
                            fw.op("act", lambda: nc.scalar.activation(out=mo[:, j, 0:n], in_=xc[:, j, 0:n], func=AF.Silu, bias=cvv[:, 2, j:j + 1], scale=cvv[:, 1, j:j + 1]), r=[xc, cvv], w=[mo])
                        fw.store("sp", mo, miw[:, 0:8, p0:p0 + n], mo[:, :, 0:n])

            def hyshort_phase():
                with fw.scope():
                    hw = fw.sb("hw", [128, 24, 3], F32)
                    hb = fw.sb("hb", [128, 24], F32)
                    fw.load("sp", hw, hw[:], hy_sw)
                    fw.load("sp", hb, hb[:], hy_sb)
                    rws = [[fw.sb("hr%d_%d" % (i, q), [128, TP], BF16) for q in range(3)] for i in range(2)]
                    cs_ = [fw.sb("hc%d" % q, [128, TP], F32) for q in range(3)]
                    uo = [fw.sb("huo%d" % i, [128, TP], BF16) for i in range(2)]
                    xo = [fw.sb("hxo%d" % i, [128, TP], BF16) for i in range(2)]
                    for o in uo + xo:
                        fw.op("pool", lambda: nc.gpsimd.memset(o[:], 0.0), w=[o])

                    def conv(e, row, ch, out):
                        eo = nc.vector if e == "dve" else nc.gpsimd
                        fw.op(e, lambda: eo.tensor_scalar(out=out[:, 1:TP - 1], in0=row[:, 0:TP - 2], scalar1=hw[:, ch, 0:1], scalar2=hb[:, ch:ch + 1], op0=ALU.mult, op1=ALU.add), r=[row, hw, hb], w=[out])
                        fw.op(e, lambda: eo.scalar_tensor_tensor(out=out[:, 1:TP - 1], in0=row[:, 1:TP - 1], scalar=hw[:, ch, 1:2], in1=out[:, 1:TP - 1], op0=ALU.mult, op1=ALU.add), r=[row, hw, out], w=[out])
                        fw.op(e, lambda: eo.scalar_tensor_tensor(out=out[:, 1:TP - 1], in0=row[:, 2:TP], scalar=hw[:, ch, 2:3], in1=out[:, 1:TP - 1], op0=ALU.mult, op1=ALU.add), r=[row, hw, out], w=[out])
                    for j in range(8):
                        r0_, r1_, rv_ = rws[j % 2]
                        fw.load("sp", r0_, r0_[:], PJ[(16 + j) * 128:(17 + j) * 128, :])
                        fw.load("sp", r1_, r1_[:], PJ[(24 + j) * 128:(25 + j) * 128, :])
                        fw.load("pool", rv_, rv_[:], PJ[(32 + j) * 128:(33 + j) * 128, :])
                        conv("dve", r0_, j, cs_[0])
                        conv("dve", r1_, 8 + j, cs_[1])
                        conv("dve", rv_, 16 + j, cs_[2])
                        fw.op("dve", lambda: nc.vector.tensor_tensor(out=uo[j % 2][:, 1:TP - 1], in0=cs_[1][:, 1:TP - 1], in1=cs_[2][:, 1:TP - 1], op=ALU.mult), r=[cs_[1], cs_[2]], w=[uo[j % 2]])
                        fw.op("act", lambda: nc.scalar.copy(out=xo[j % 2][:, 1:TP - 1], in_=cs_[0][:, 1:TP - 1]), r=[cs_[0]], w=[xo[j % 2]])
                        fw.store("sp", uo[j % 2], UU[j * 128:(j + 1) * 128, :], uo[j % 2][:])
                        fw.store("sp", xo[j % 2], X0[j * 128:(j + 1) * 128, :], xo[j % 2][:])

            def hyena_phase(L, seqs):
                C = HYC[L]
                TC = L // 128
                FC = TC
                NB = min(512, L)
                NBN = L // NB
                HS = C["HS"]
                with fw.scope():
                    hs = fw.sb("hs", [128, TC, 1024], BF16)
                    hd = fw.sb("hd", [128, TC, 1024], BF16)
                    with fw.scope():
                        w1t = fw.sb("w1t", [33, 64], F32)
                        w2t = fw.sb("w2t", [64, 64], F32)
                        w3t = fw.sb("w3t", [64, 2048], F32)
                        hv = fw.sb("hv", [64, 4], F32)
                        sc = fw.sb("hsc", [64, 8], F32)
                        zt = fw.sb("zt", [33, L], F32)
                        hid2 = fw.sb("hid2", [64, L], F32)
                        negt = fw.sb("negt", [128, TC], F32)
                        adl = fw.sb("adl", [128, 1024], F32)
                        for (t_, d_) in ((w1t, hy_w1), (w2t, hy_w2), (w3t, hy_w3), (hv, hy_v), (zt, C["zT"]), (negt, C["negt"]), (adl, absdel)):
                            fw.load("sp", t_, t_[:], d_)
                        for li in range(2):
                            fcol = hv[:, 2 * li + 1:2 * li + 2]
                            bcol = hv[:, 2 * li:2 * li + 1]
                            fw.op("dve", lambda: nc.vector.tensor_scalar(out=sc[:, 4 * li:4 * li + 1], in0=fcol, scalar1=0.25, scalar2=None, op0=ALU.mult), r=[hv], w=[sc])
                            fw.op("dve", lambda: nc.vector.scalar_tensor_tensor(out=sc[:, 4 * li + 1:4 * li + 2], in0=fcol, scalar=0.25, in1=bcol, op0=ALU.mult, op1=ALU.mult), r=[hv], w=[sc])
                            fw.op("dve", lambda: nc.vector.tensor_scalar(out=sc[:, 4 * li + 2:4 * li + 3], in0=fcol, scalar1=0.125, scalar2=None, op0=ALU.mult), r=[hv], w=[sc])
                            fw.op("dve", lambda: nc.vector.scalar_tensor_tensor(out=sc[:, 4 * li + 3:4 * li + 4], in0=fcol, scalar=0.125, in1=bcol, op0=ALU.mult, op1=ALU.mult), r=[hv], w=[sc])
                        s4 = fw.sb("s4", [64, 512], F32)
                        s8 = fw.sb("s8", [64, 512], F32)
                        tq = fw.sb("tq", [64, 512], F32)
                        c4 = fw.sb("c4", [64, 512], F32)
                        h1b = fw.sb("h1b", [64, 512], F32)

                        def sin_big(ps, li, out_ap, n, wb):
                            fw.op("act", lambda: nc.scalar.activation(out=s4[:, 0:n], in_=ps[0:64, 0:n], func=AF.Sin, bias=sc[:, 4 * li + 1:4 * li + 2], scale=sc[:, 4 * li:4 * li + 1]), r=[ps, sc], w=[s4])
                            fw.op("act", lambda: nc.scalar.activation(out=s8[:, 0:n], in_=ps[0:64, 0:n], func=AF.Sin, bias=sc[:, 4 * li + 3:4 * li + 4], scale=sc[:, 4 * li + 2:4 * li + 3]), r=[ps, sc], w=[s8])
                            fw.op("dve", lambda: nc.vector.tensor_tensor(out=tq[:, 0:n], in0=s8[:, 0:n], in1=s8[:, 0:n], op=ALU.mult), r=[s8], w=[tq])
                            fw.op("dve", lambda: nc.vector.tensor_scalar(out=c4[:, 0:n], in0=tq[:, 0:n], scalar1=-2.0, scalar2=1.0, op0=ALU.mult, op1=ALU.add), r=[tq], w=[c4])
                            fw.op("dve", lambda: nc.vector.scalar_tensor_tensor(out=c4[:, 0:n], in0=s4[:, 0:n], scalar=2.0, in1=c4[:, 0:n], op0=ALU.mult, op1=ALU.mult), r=[s4, c4], w=[c4])
                            fw.op("dve", lambda: nc.vector.tensor_tensor(out=tq[:, 0:n], in0=s4[:, 0:n], in1=s4[:, 0:n], op=ALU.mult), r=[s4], w=[tq])
                            fw.op("dve", lambda: nc.vector.tensor_scalar(out=tq[:, 0:n], in0=tq[:, 0:n], scalar1=-2.0, scalar2=1.0, op0=ALU.mult, op1=ALU.add), r=[tq], w=[tq])
                            fw.op("dve", lambda: nc.vector.scalar_tensor_tensor(out=out_ap, in0=c4[:, 0:n], scalar=2.0, in1=tq[:, 0:n], op0=ALU.mult, op1=ALU.mult), r=[c4, tq], w=[wb])
                        for blk in range(L // NB):
                            n = NB
                            ps = fw.psum()
                            fw.mm(ps, ps[0:64, 0:n], [(w1t[0:33, :], zt[0:33, blk * NB:(blk + 1) * NB])], [w1t, zt])
                            sin_big(ps, 0, h1b[:, 0:n], n, h1b)
                            ps2 = fw.psum()
                            fw.mm(ps2, ps2[0:64, 0:n], [(w2t[:, :], h1b[:, 0:n])], [w2t, h1b])
                            sin_big(ps2, 1, hid2[:, blk * NB:(blk + 1) * NB], n, hid2)
                        dk = fw.sb("dk", [128, 1024], F32)
                        fd = fw.sb("fd", [128, 1024], F32)
                        bd = fw.sb("bd", [128, 1024], F32)
                        for tb in range(TC):
                            pbs = [fw.psum() for _ in range(4)]
                            for c in range(4):
                                fw.mm(pbs[c], pbs[c][:, :], [(hid2[0:64, tb * 128:(tb + 1) * 128], w3t[0:64, c * 512:(c + 1) * 512])], [hid2, w3t])
                            fw.op("act", lambda: nc.scalar.activation(out=dk[:], in_=adl[:], func=AF.Exp, scale=negt[:, tb:tb + 1]), r=[adl, negt], w=[dk])
                            for c in range(2):
                                fw.op("dve", lambda: nc.vector.tensor_tensor(out=fd[:, c * 512:(c + 1) * 512], in0=pbs[c][:, :], in1=dk[:, c * 512:(c + 1) * 512], op=ALU.mult), r=[pbs[c], dk], w=[fd])
                                fw.op("dve", lambda: nc.vector.tensor_tensor(out=bd[:, c * 512:(c + 1) * 512], in0=pbs[2 + c][:, :], in1=dk[:, c * 512:(c + 1) * 512], op=ALU.mult), r=[pbs[2 + c], dk], w=[bd])
                            if tb == 0:
                                fw.op("dve", lambda: nc.vector.memset(bd[0:1, :], 0.0), w=[bd])
                            fw.op("pool", lambda: nc.gpsimd.tensor_tensor(out=hs[:, tb, :], in0=fd[:], in1=bd[:], op=ALU.add), r=[fd, bd], w=[hs])
                            fw.op("pool", lambda: nc.gpsimd.tensor_tensor(out=hd[:, tb, :], in0=fd[:], in1=bd[:], op=ALU.subtract), r=[fd, bd], w=[hd])
                    with fw.scope():
                        fcts = [fw.sb("fct%d" % i, [128, TC, 128], BF16) for i in range(2)]
                        fsts = [fw.sb("fst%d" % i, [128, TC, 128], BF16) for i in range(2)]
                        hsts = [fw.sb("hst%d" % i, [128, 2, 1024], F32) for i in range(2)]
                        for fc in range(FC):
                            fct = fcts[fc % 2]
                            fst = fsts[fc % 2]
                            hst = hsts[fc % 2]
                            fw.load("sp", fct, fct[:], C["fwc"][fc])
                            fw.load("pool", fst, fst[:], C["fws"][fc])
                            for ri, (tab, hx) in enumerate(((fct, hs), (fst, hd))):
                                for h in range(2):
                                    pb = fw.psum()
                                    fw.mm(pb, pb[:, :], [(tab[:, tc, :], hx[:, tc, h * 512:(h + 1) * 512]) for tc in range(TC)], [tab, hx])
                                    evac(hst[:, ri, h * 512:(h + 1) * 512], pb[:, :], [pb], [hst], fc)
                            fw.store("sp", hst, HS[:, fc * 128:(fc + 1) * 128, :].rearrange("r p c -> p r c"), hst[:])
                for s0 in seqs:
                    sp0 = ppos(s0)
                    with fw.scope():
                        ut = fw.sb("ut", [128, TC, 1024], BF16)
                        with fw.scope():
                            idt = fw.sb("idt", [128, 128], F32)
                            fw.load("sp", idt, idt[:], identf)
                            urs = [fw.sb("ur%d" % i, [128, L], F32) for i in range(2)]
                            for j in range(8):
                                ur = urs[j % 2]
                                fw.load("pool", ur, ur[:], UU[j * 128:(j + 1) * 128, sp0:sp0 + L])
                                nq = min(4, TC)
                                for g in range(TC // nq):
                                    pb = fw.psum()
                                    for q in range(nq):
                                        fw.transpose(pb, pb[:, q * 128:(q + 1) * 128], ur[:, (g * nq + q) * 128:(g * nq + q + 1) * 128], idt[:], [ur, idt])
                                    evac(ut[:, g * nq:(g + 1) * nq, j * 128:(j + 1) * 128], pb[:, 0:nq * 128].rearrange("p (q c) -> p q c", c=128), [pb], [ut], 0)
                        fcts = [fw.sb("gct%d" % i, [128, TC, 128], BF16) for i in range(2)]
                        fsts = [fw.sb("gst%d" % i, [128, TC, 128], BF16) for i in range(2)]
                        hts = [fw.sb("hts%d" % i, [128, 2, 1024], F32) for i in range(2)]
                        ysts = [fw.sb("yst%d" % i, [128, 2, 1024], BF16) for i in range(2)]
                        t1 = fw.sb("t1", [128, 512], F32)
                        t2 = fw.sb("t2", [128, 512], F32)
                        t3 = fw.sb("t3", [128, 512], F32)
                        t4 = fw.sb("t4", [128, 512], F32)
                        for fc in range(FC):
                            fct = fcts[fc % 2]
                            fst = fsts[fc % 2]
                            ht = hts[fc % 2]
                            yst = ysts[fc % 2]
                            fw.load("sp", fct, fct[:], C["fwc"][fc])
                            fw.load("pool", fst, fst[:], C["fws"][fc])
                            fw.load("sp", ht, ht[:], HS[:, fc * 128:(fc + 1) * 128, :].rearrange("r p c -> p r c"))
                            for h in range(2):
                                hsl = slice(h * 512, (h + 1) * 512)
                                pr = fw.psum()
                                pi = fw.psum()
                                fw.mm(pr, pr[:, :], [(fct[:, tc, :], ut[:, tc, hsl]) for tc in range(TC)], [fct, ut])
                                fw.mm(pi, pi[:, :], [(fst[:, tc, :], ut[:, tc, hsl]) for tc in range(TC)], [fst, ut])
                                fw.op("dve", lambda: nc.vector.tensor_tensor(out=t1[:], in0=pr[:, :], in1=ht[:, 0, hsl], op=ALU.mult), r=[pr, ht], w=[t1])
                                fw.op("dve", lambda: nc.vector.tensor_tensor(out=t3[:], in0=pr[:, :], in1=ht[:, 1, hsl], op=ALU.mult), r=[pr, ht], w=[t3])
                                fw.op("dve", lambda: nc.vector.tensor_tensor(out=t2[:], in0=pi[:, :], in1=ht[:, 1, hsl], op=ALU.mult), r=[pi, ht], w=[t2])
                                fw.op("dve", lambda: nc.vector.tensor_tensor(out=t4[:], in0=pi[:, :], in1=ht[:, 0, hsl], op=ALU.mult), r=[pi, ht], w=[t4])
                                fw.op("pool", lambda: nc.gpsimd.tensor_tensor(out=yst[:, 0, hsl], in0=t1[:], in1=t2[:], op=ALU.subtract), r=[t1, t2], w=[yst])
                                fw.op("pool", lambda: nc.gpsimd.tensor_tensor(out=yst[:, 1, hsl], in0=t3[:], in1=t4[:], op=ALU.add), r=[t3, t4], w=[yst])
                            for ri in range(2):
                                fw.store("sp", yst, YD[:, ri, :, fc, :].rearrange("j p c -> p j c"), yst[:, ri, :].rearrange("p (j c) -> p j c", c=128))
                    if L == 4096:
                        with fw.scope():
                            hbt = fw.sb("hbt", [128, 8], F32)
                            qs = fw.sb("hqs", [128, 4], F32)
                            fw.load("sp", hbt, hbt[:], hy_bias)
                            fw.load("sp", qs, qs[:], qsel)
                            x0q = fw.sb("x0q", [128, 8, 1026], BF16)
                            uq = fw.sb("uq", [128, 8, 1026], BF16)
                            with fw.scope():
                                frs = [fw.sb("frow%d" % i, [128, TP], BF16) for i in range(2)]
                                k = 0
                                for j in range(8):
                                    for (srcT, dstq) in ((X0, x0q), (UU, uq)):
                                        fr = frs[k % 2]
                                        k += 1
                                        fw.load("sp", fr, fr[:], srcT[j * 128:(j + 1) * 128, :])
                                        fw.op("dve", lambda: nc.vector.tensor_scalar(out=dstq[:, j, :], in0=fr[:, 559:559 + 1026], scalar1=qs[:, 0:1], scalar2=None, op0=ALU.mult), r=[fr, qs], w=[dstq])
                                        for q in range(1, 4):
                                            fw.op("dve", lambda: nc.vector.scalar_tensor_tensor(out=dstq[:, j, :], in0=fr[:, 559 + q * 1024:559 + q * 1024 + 1026], scalar=qs[:, q:q + 1], in1=dstq[:, j, :], op0=ALU.mult, op1=ALU.add), r=[fr, qs, dstq], w=[dstq])
                            gct = fw.sb("qct", [128, FC, 512], BF16)
                            gst = fw.sb("qst", [128, FC, 512], BF16)
                            ght = fw.sb("qht", [128, 2, FC, 2], BF16)
                            fw.load("sp", ght, ght[:], ivh.rearrange("r p f n -> p r f n"))
                            yrts = [fw.sb("yrt%d" % i, [128, FC, 128], BF16) for i in range(2)]
                            yits = [fw.sb("yit%d" % i, [128, FC, 128], BF16) for i in range(2)]
                            ubs = [fw.sb("ub%d" % i, [128, 512], F32) for i in range(2)]
                            zzs = [fw.sb("zz%d" % i, [128, 512], BF16) for i in range(2)]
                            k = 0
                            blocks = [(0, 512, [(1, 561, 0, 512)]), (1, 512, [(513, 1073, 0, 512)]), (None, 2, [(0, 560, 0, 1), (1025, 1585, 1, 1)])]
                            for (bi, n, parts) in blocks:
                                if bi is not None:
                                    fw.load("sp", gct, gct[:], ivcq[bi])
                                    fw.load("pool", gst, gst[:], ivsq[bi])
                                for j in range(8):
                                    yrt, yit, ub, zz = yrts[k % 2], yits[k % 2], ubs[k % 2], zzs[k % 2]
                                    k += 1
                                    fw.load("sp", yrt, yrt[:], YD[j, 0])
                                    fw.load("pool", yit, yit[:], YD[j, 1])
                                    pb = fw.psum()
                                    if bi is not None:
                                        pairs = [(yrt[:, fc, :], gct[:, fc, :]) for fc in range(FC)] + [(yit[:, fc, :], gst[:, fc, :]) for fc in range(FC)]
                                        rd = [yrt, yit, gct, gst]
                                    else:
                                        pairs = [(yrt[:, fc, :], ght[:, 0, fc, :]) for fc in range(FC)] + [(yit[:, fc, :], ght[:, 1, fc, :]) for fc in range(FC)]
                                        rd = [yrt, yit, ght]
                                    fw.mm(pb, pb[:, 0:n], pairs, rd)
                                    for (sc, dc, pc, wd) in parts:
                                        fw.op("dve", lambda: nc.vector.tensor_scalar(out=ub[:, pc:pc + wd], in0=uq[:, j, sc:sc + wd], scalar1=hbt[:, j:j + 1], scalar2=None, op0=ALU.mult), r=[uq, hbt], w=[ub])
                                        fw.op("dve", lambda: nc.vector.scalar_tensor_tensor(out=ub[:, pc:pc + wd], in0=pb[:, pc:pc + wd], scalar=1.0 / L, in1=ub[:, pc:pc + wd], op0=ALU.mult, op1=ALU.add), r=[pb, ub], w=[ub])
                                        fw.op("pool", lambda: nc.gpsimd.tensor_tensor(out=zz[:, pc:pc + wd], in0=ub[:, pc:pc + wd], in1=x0q[:, j, sc:sc + wd], op=ALU.mult), r=[ub, x0q], w=[zz])
                                        fw.store("sp", zz, MI2[1024 + j * 128:1024 + (j + 1) * 128, dc:dc + wd], zz[:, pc:pc + wd])
                        continue
                    with fw.scope():
                        hbt = fw.sb("hbt", [128, 8], F32)
                        fw.load("sp", hbt, hbt[:], hy_bias)
                        gcts = [fw.sb("ict%d" % i, [128, FC, NB], BF16) for i in range(2)]
                        gsts = [fw.sb("ist%d" % i, [128, FC, NB], BF16) for i in range(2)]
                        yrts = [fw.sb("yrt%d" % i, [128, FC, 128], BF16) for i in range(2)]
                        yits = [fw.sb("yit%d" % i, [128, FC, 128], BF16) for i in range(2)]
                        x0ts = [fw.sb("x0t%d" % i, [128, NB], BF16) for i in range(2)]
                        u2ts = [fw.sb("u2t%d" % i, [128, NB], BF16) for i in range(2)]
                        ubs = [fw.sb("ub%d" % i, [128, NB], F32) for i in range(2)]
                        zzs = [fw.sb("zz%d" % i, [128, NB], BF16) for i in range(2)]
                        k = 0
                        for nb in range(NBN):
                            gct = gcts[nb % 2]
                            gst = gsts[nb % 2]
                            fw.load("sp", gct, gct[:], C["ivc"][nb])
                            fw.load("pool", gst, gst[:], C["ivs"][nb])
                            c0 = sp0 + nb * NB
                            for j in range(8):
                                yrt, yit, x0t, u2t, ub, zz = yrts[k % 2], yits[k % 2], x0ts[k % 2], u2ts[k % 2], ubs[k % 2], zzs[k % 2]
                                k += 1
                                fw.load("sp", yrt, yrt[:, 0:FC, :], YD[j, 0, :, 0:FC, :])
                                fw.load("pool", yit, yit[:, 0:FC, :], YD[j, 1, :, 0:FC, :])
                                fw.load("sp", x0t, x0t[:], X0[j * 128:(j + 1) * 128, c0:c0 + NB])
                                fw.load("sp", u2t, u2t[:], UU[j * 128:(j + 1) * 128, c0:c0 + NB])
                                pb = fw.psum()
                                fw.mm(pb, pb[:, 0:NB], [(yrt[:, fc, :], gct[:, fc, :]) for fc in range(FC)] + [(yit[:, fc, :], gst[:, fc, :]) for fc in range(FC)], [yrt, yit, gct, gst])
                                fw.op("dve", lambda: nc.vector.tensor_scalar(out=ub[:], in0=u2t[:], scalar1=hbt[:, j:j + 1], scalar2=None, op0=ALU.mult), r=[u2t, hbt], w=[ub])
                                fw.op("dve", lambda: nc.vector.scalar_tensor_tensor(out=ub[:], in0=pb[:, 0:NB], scalar=1.0 / L, in1=ub[:], op0=ALU.mult, op1=ALU.add), r=[pb, ub], w=[ub])
                                fw.op("pool", lambda: nc.gpsimd.tensor_tensor(out=zz[:], in0=ub[:], in1=x0t[:], op=ALU.mult), r=[ub, x0t], w=[zz])
                                fw.store("sp", zz, MI2[1024 + j * 128:1024 + (j + 1) * 128, c0:c0 + NB], zz[:])

            def odd_mixer_phase(x_src):
                with fw.scope():
                    hT = fw.sb("hT1", [128, 16, T], BF16)
                    norm_phase(x_src, modA1[1], modB1[1], hT)
                    with fw.scope():
                        wts = [fw.sb("wq%d" % i, [128, 16, 512], BF16) for i in range(2)]
                        rows = make_rows(2)
                        proj_rows(hT, 16, o_w_in, 0, 5120, PJ, 0, wts, rows)
                if upto >= 6.2:
                    conformer_phase()
                if upto >= 6.4:
                    hyshort_phase()
                if upto >= 6.6:
                    hyena_phase(256, [0, 256])
                if upto >= 6.8:
                    hyena_phase(4096, [512])

            if upto >= 3:
                attention_phase()
            if upto >= 4:
                wout_phase(e_w_out, xT, xA if upto >= 5 else yT, modG1[0])
            if upto >= 5:
                ffn_phase(0, xA, xB if upto >= 6 else yT)
            if upto >= 6:
                odd_mixer_phase(xB)
            if 7 <= upto < 8:
                wout_phase(o_w_out, xB, yT, modG1[1])
            if upto >= 8:
                T2 = GEO_TAIL["T"]
                with fw.scope():
                    mTq = fw.sb("mTq", [128, 16, T2], BF16)
                    with fw.scope():
                        qs = fw.sb("qs", [128, 4], F32)
                        fw.load("sp", qs, qs[:], qsel)
                        set_geo(GEO_TAIL)
                        load_act_from(MI2, mTq)
                        set_geo(GEO_FULL)
                        xrs = [fw.sb("xrow%d" % i, [128, 4609], F32) for i in range(2)]
                        xqs = [fw.sb("xq%d" % i, [128, T2], F32) for i in range(2)]
                        for xr_ in xrs:
                            fw.op("pool", lambda: nc.gpsimd.memset(xr_[:, 4608:4609], 0.0), w=[xr_])
                        for kc in range(16):
                            xr_ = xrs[kc % 2]
                            xq_ = xqs[kc % 2]
                            fw.load("sp", xr_, xr_[:, 0:4608], xB[kc * 128:(kc + 1) * 128, :])
                            fw.op("act", lambda: nc.scalar.copy(out=xq_[:, 0:512], in_=xr_[:, 0:512]), r=[xr_], w=[xq_])
                            fw.op("dve", lambda: nc.vector.tensor_scalar(out=xq_[:, 512:T2], in0=xr_[:, 511:511 + 1026], scalar1=qs[:, 0:1], scalar2=None, op0=ALU.mult), r=[xr_, qs], w=[xq_])
                            for q in range(1, 4):
                                fw.op("dve", lambda: nc.vector.scalar_tensor_tensor(out=xq_[:, 512:T2], in0=xr_[:, 511 + q * 1024:511 + q * 1024 + 1026], scalar=qs[:, q:q + 1], in1=xq_[:, 512:T2], op0=ALU.mult, op1=ALU.add), r=[xr_, qs, xq_], w=[xq_])
                            fw.store("sp", xq_, xQ[kc * 128:(kc + 1) * 128, :], xq_[:])
                    set_geo(GEO_TAIL)
                    with fw.scope():
                        wts = [fw.sb("wo2_%d" % i, [128, 16, 512], BF16) for i in range(2)]
                        proj_resid(mTq, 16, o_w_out, xQ, xA2, modG1[1], wts)
                with fw.scope():
                    hm = fw.sb("hm", [128, 2], F32)
                    fw.load("sp", hm, hm[:], hmask)
                    ffn_phase(1, xA2, yQ, PJ2, FF2, hm)
                set_geo(GEO_FULL)

            fw.barrier()
    return nc


def _chunks(vec, n):
    return np.ascontiguousarray(np.asarray(vec, np.float32).reshape(n, 128).T)


_CONST_CACHE = {}


def _consts():
    if _CONST_CACHE:
        return _CONST_CACHE
    f = np.float32
    out = {}
    max_decay = math.log(1e-2) / 0.3
    min_decay = math.log(1e-2) / 1.5
    deltas = np.linspace(min_decay, max_decay, 1024, dtype=f)
    out["absdel"] = np.ascontiguousarray(np.broadcast_to(np.abs(deltas)[None, :], (128, 1024)), dtype=f)
    out["identf"] = np.eye(128, dtype=f)
    for L in (256, 4096):
        t = np.linspace(0.0, 1.0, L, dtype=f)
        ang = (2 * math.pi * np.arange(L, dtype=f) / L).astype(f)
        freqs = np.linspace(1e-4, 15, 16, dtype=f)
        z = np.concatenate([t[:, None], np.cos(freqs[None, :] * ang[:, None]), -np.sin(freqs[None, :] * ang[:, None])], axis=-1).astype(f)
        out["zT%d" % L] = np.ascontiguousarray(z.T)
        out["negt%d" % L] = np.ascontiguousarray((-t).reshape(L // 128, 128).T, dtype=f)
        tt = np.arange(L, dtype=np.int64)[:, None]
        ff = np.arange(L, dtype=np.int64)[None, :]
        ph = ((2 * ff + 1) * tt) % (4 * L)
        angm = ph.astype(np.float64) * (2 * math.pi / (4 * L))
        TCn = L // 128
        NB = min(512, L)
        for nm, M in (("c", np.cos(angm)), ("s", np.sin(angm))):
            Mb = M.astype(ml_dtypes.bfloat16)
            out["fw%s%d" % (nm, L)] = np.ascontiguousarray(Mb.reshape(TCn, 128, TCn, 128).transpose(2, 1, 0, 3))
            out["iv%s%d" % (nm, L)] = np.ascontiguousarray(Mb.reshape(L // NB, NB, TCn, 128).transpose(0, 3, 2, 1))
    _CONST_CACHE.update(out)
    return _CONST_CACHE


def prep_inputs(inp, core):
    f = np.float32
    sbi = core // 4
    p0, p1 = 2 * core, 2 * core + 1
    xp, xs = inp["x_prompt"], inp["x_sample"]
    xT = np.ascontiguousarray(np.concatenate([xp[p0].T, xp[p1].T, xs[sbi].T], axis=1), dtype=f)
    cs = np.stack([inp["c_ctx"], inp["c"][sbi]], axis=-1)
    csil = np.ascontiguousarray(cs.reshape(16, 128, 2).transpose(1, 0, 2).reshape(128, 32), dtype=f)
    adab = np.ascontiguousarray(inp["ada_b"].reshape(2, 96, 128).transpose(2, 0, 1), dtype=f)
    nmix = np.ascontiguousarray(inp["norm_mix"].reshape(2, 16, 128).transpose(2, 0, 1), dtype=f)
    nffn = np.ascontiguousarray(inp["norm_ffn"].reshape(2, 16, 128).transpose(2, 0, 1), dtype=f)
    conv_a = np.ascontiguousarray(inp["e_conv_a"][0].reshape(3, 8, 128).transpose(2, 1, 0), dtype=f)
    qkn = np.ascontiguousarray(np.stack([inp["e_q_norm"][0], inp["e_k_norm"][0]], axis=-1), dtype=f)
    rpbp = np.zeros((8, 15, 127), f)
    rpbp[:, :, 48:79] = inp["e_rpb"][0]
    ckT = np.ascontiguousarray(inp["cache_k"][sbi, 0].transpose(1, 2, 0), dtype=f)
    cv = np.ascontiguousarray(inp["cache_v"][sbi, 0].reshape(512, 1024), dtype=f)
    ffn_conv = np.ascontiguousarray(inp["ffn_conv"].reshape(2, 3, 44, 128).transpose(3, 0, 2, 1), dtype=f)
    j64 = np.ascontiguousarray(np.eye(64, dtype=f)[::-1])
    kc = np.arange(64)[:, None]
    qc = np.arange(64)[None, :]
    cst = np.clip(qc - 8, 0, 48)
    cm = ((kc >= cst) & (kc < cst + 16)).astype(f)
    colmask = np.ascontiguousarray(np.concatenate([cm, cm], axis=0))
    extra = dict(_consts())
    extra["conf_dw"] = np.ascontiguousarray(inp["o_conf_dw"][0].reshape(31, 8, 128).transpose(2, 1, 0), dtype=f)
    extra["conf_v"] = np.ascontiguousarray(np.stack([inp["o_conf_dw_b"][0], inp["o_conf_ln_g"][0], inp["o_conf_ln_b"][0]], 0).reshape(3, 8, 128).transpose(2, 0, 1), dtype=f)
    extra["hy_sw"] = np.ascontiguousarray(inp["o_hy_short"][0].reshape(3, 24, 128).transpose(2, 1, 0), dtype=f)
    extra["hy_sb"] = _chunks(inp["o_hy_short_b"][0], 24)
    extra["hy_w1"] = np.ascontiguousarray(inp["o_hy_w1"][0], dtype=f)
    extra["hy_w2"] = np.ascontiguousarray(inp["o_hy_w2"][0], dtype=f)
    extra["hy_w3"] = np.ascontiguousarray(inp["o_hy_w3"][0], dtype=f)
    extra["hy_v"] = np.ascontiguousarray(np.stack([inp["o_hy_b1"][0], inp["o_hy_f1"][0], inp["o_hy_b2"][0], inp["o_hy_f2"][0]], -1), dtype=f)
    extra["hy_bias"] = _chunks(inp["o_hy_bias"][0], 8)
    r_ = core % 4
    cst = _consts()
    extra["ivcq"] = np.ascontiguousarray(cst["ivc4096"][2 * r_:2 * r_ + 2])
    extra["ivsq"] = np.ascontiguousarray(cst["ivs4096"][2 * r_:2 * r_ + 2])
    hl = max(r_ * 1024 - 1, 0)
    hr = min((r_ + 1) * 1024, 4095)
    ivh_ = np.zeros((2, 128, 32, 2), ml_dtypes.bfloat16)
    for ci_, nm_ in enumerate(("ivc4096", "ivs4096")):
        for hi_, n_ in enumerate((hl, hr)):
            ivh_[ci_, :, :, hi_] = cst[nm_][n_ // 512, :, :, n_ % 512]
    extra["ivh"] = ivh_
    extra["qsel"] = np.ascontiguousarray(np.broadcast_to(np.eye(4, dtype=f)[r_][None, :], (128, 4)))
    extra["hmask"] = np.ascontiguousarray(np.broadcast_to(np.array([0.0 if r_ == 0 else 1.0, 0.0 if r_ == 3 else 1.0], f)[None, :], (128, 2)))
    return {
        **extra,
        "xT": xT, "csil": csil, "ada_w": inp["ada_w"], "adab": adab, "nmix": nmix, "nffn": nffn,
        "e_w_in": inp["e_w_in"][0], "e_w_out": inp["e_w_out"][0], "o_w_in": inp["o_w_in"][0], "o_w_out": inp["o_w_out"][0],
        "ffn_in": inp["ffn_in"], "ffn_out": inp["ffn_out"], "conv_a": conv_a, "qkn": qkn, "rpbp": rpbp,
        "ckT": ckT, "cv": cv, "ffn_conv": ffn_conv, "j64": j64, "colmask": colmask,
    }


def kernel(**inputs):
    inp = {k: np.asarray(v) for k, v in inputs.items()}
    nc = build()
    in_maps = [prep_inputs(inp, c) for c in range(NCORES)]
    res = run_bass_kernel_spmd(nc, in_maps, core_ids=list(range(NCORES)))
    R = res.results
    y_prompt = np.zeros((16, 256, D), np.float32)
    y_sample = np.zeros((2, 4096, D), np.float32)
    nk = np.zeros((16, 1, 256, 8, 128), np.float32)
    nv = np.zeros((16, 1, 256, 8, 128), np.float32)
    for c in range(NCORES):
        yQ = R[c]["yQ"]
        for s in range(2):
            y_prompt[2 * c + s] = yQ[:, s * 256:(s + 1) * 256].T
            nk[2 * c + s, 0] = R[c]["kT_out"][:, s * 256:(s + 1) * 256].T.reshape(256, 8, 128)
            nv[2 * c + s, 0] = R[c]["v_out"][s * 256:(s + 1) * 256].reshape(256, 8, 128)
        r_ = c % 4
        y_sample[c // 4, r_ * 1024:(r_ + 1) * 1024] = yQ[:, 513:1537].T
    return (y_prompt, y_sample, nk, nv)
```

```python
import math
from contextlib import ExitStack

import numpy as np
import ml_dtypes
import concourse.bass as bass
import concourse.mybir as mybir
from concourse.ap import AP
from concourse.bass_utils import run_bass_kernel_spmd

F32 = mybir.dt.float32
BF16 = mybir.dt.bfloat16
AF = mybir.ActivationFunctionType
ALU = mybir.AluOpType

D = 2048
T = 4608
PAD = 16
SEGS = [(0, 256), (256, 256), (512, 4096)]
TP = T + PAD * (len(SEGS) + 1)
TILES = [(0, 256), (256, 256)] + [(512 + 512 * i, 512) for i in range(8)]
EPS = 1e-6
NCORES = 8


GEO_FULL = dict(T=4608, SEGS=[(0, 256), (256, 256), (512, 4096)], TILES=[(0, 256), (256, 256)] + [(512 + 512 * i, 512) for i in range(8)])
GEO_TAIL = dict(T=1538, SEGS=[(0, 256), (256, 256), (512, 1026)], TILES=[(0, 256), (256, 256), (512, 1), (1537, 1), (513, 512), (1025, 512)])
TP_TAIL = 1538 + 4 * PAD
GEO_C = dict(T=1568, SEGS=[(0, 256), (256, 256), (512, 1056)], TILES=[])
TP_C = 1568 + 4 * PAD


def set_geo(g):
    global T, SEGS, TILES, TP
    T = g["T"]
    SEGS = g["SEGS"]
    TILES = g["TILES"]
    TP = T + PAD * (len(SEGS) + 1)


def ppos(t):
    for si, (s0, ln) in enumerate(SEGS):
        if s0 <= t < s0 + ln:
            return t + PAD * (si + 1)
    raise ValueError(t)


def tile_v(t0):
    return 0 if t0 < 512 else 1


class Buf:
    def __init__(self, t, name):
        self.t = t
        self.name = name
        self.w = {}
        self.r = {}
        self.dsem = None
        self.psum = name.startswith("ps")
        self.pend = 0

    def __getitem__(self, k):
        return self.t[k]


class Eng:
    def __init__(self, obj, sem, idx):
        self.obj = obj
        self.sem = sem
        self.idx = idx
        self.count = 0
        self.seen = {}


class FW:
    def __init__(self, nc, es, ndsem=40):
        self.nc = nc
        self.sems = []
        self.counts = []
        self.E = {}
        for name, obj in (("pe", nc.tensor), ("act", nc.scalar), ("dve", nc.vector), ("pool", nc.gpsimd), ("sp", nc.sync)):
            s = es.enter_context(nc.semaphore("s_" + name))
            self.sems.append(s)
            self.counts.append(0)
            self.E[name] = Eng(obj, s, len(self.sems) - 1)
        self.free_dsems = []
        for i in range(ndsem):
            s = es.enter_context(nc.semaphore("d_%d" % i))
            self.sems.append(s)
            self.counts.append(0)
            self.free_dsems.append(len(self.sems) - 1)
        self.ps = []
        for i in range(8):
            t = es.enter_context(nc.psum_tensor("ps%d" % i, [128, 512], F32))
            self.ps.append(Buf(t, "ps%d" % i))
        self.ps_i = 0
        self.scopes = []
        self.pending = []
        self.max_pending = 6

    def scope(self):
        fw = self

        class _S:
            def __enter__(s):
                s.es = ExitStack()
                s.es.__enter__()
                s.bufs = []
                fw.scopes.append(s)
                return s

            def __exit__(s, *a):
                fw.barrier()
                for b in s.bufs:
                    if b.dsem is not None:
                        fw.free_dsems.append(b.dsem)
                        b.dsem = None
                fw.scopes.pop()
                return s.es.__exit__(*a)
        return _S()

    def sb(self, name, shape, dtype):
        s = self.scopes[-1]
        self.uid = getattr(self, "uid", 0) + 1
        name = "%s_u%d" % (name, self.uid)
        t = s.es.enter_context(self.nc.sbuf_tensor(name, list(shape), dtype))
        b = Buf(t, name)
        s.bufs.append(b)
        return b

    def psum(self):
        b = self.ps[self.ps_i]
        self.ps_i = (self.ps_i + 1) % 8
        return b

    def _wait(self, e, toks):
        E = self.E[e]
        for si, val in toks.items():
            if e == "pe" and si == E.idx:
                continue
            if E.seen.get(si, 0) < val:
                E.obj.wait_ge(self.sems[si], val)
                E.seen[si] = val

    @staticmethod
    def _merge(d, o):
        for k, v in o.items():
            if d.get(k, 0) < v:
                d[k] = v

    def _flush_for(self, bufs):
        while any(b.pend for b in bufs):
            self._emit_store(*self.pending.pop(0))

    def flush_all(self):
        while self.pending:
            self._emit_store(*self.pending.pop(0))

    def op(self, e, fn, r=(), w=()):
        self._flush_for(w)
        E = self.E[e]
        toks = {}
        for b in r:
            self._merge(toks, b.w)
            if b.psum:
                self._merge(toks, b.r)
        for b in w:
            self._merge(toks, b.w)
            self._merge(toks, b.r)
        self._wait(e, toks)
        ins = fn()
        self.counts[E.idx] += 1
        ins.then_inc(E.sem, 1)
        tok = {E.idx: self.counts[E.idx]}
        for b in w:
            b.w = dict(tok)
            b.r = {}
        for b in r:
            if b not in w:
                self._merge(b.r, tok)
        return ins

    def mm(self, psb, out_ap, pairs, reads):
        E = self.E["pe"]
        toks = {}
        for b in reads:
            self._merge(toks, b.w)
        self._merge(toks, psb.w)
        self._merge(toks, psb.r)
        self._wait("pe", toks)
        n = len(pairs)
        for i, (l, rr) in enumerate(pairs):
            ins = self.nc.tensor.matmul(out_ap, l, rr, start=(i == 0), stop=(i == n - 1))
        self.counts[E.idx] += 1
        ins.then_inc(E.sem, 1)
        tok = {E.idx: self.counts[E.idx]}
        psb.w = dict(tok)
        psb.r = {}
        for b in reads:
            self._merge(b.r, tok)

    def transpose(self, psb, out_ap, in_ap, ident_ap, reads):
        E = self.E["pe"]
        toks = {}
        for b in reads:
            self._merge(toks, b.w)
        self._merge(toks, psb.w)
        self._merge(toks, psb.r)
        self._wait("pe", toks)
        ins = self.nc.tensor.transpose(out_ap, in_ap, ident_ap)
        self.counts[E.idx] += 1
        ins.then_inc(E.sem, 1)
        tok = {E.idx: self.counts[E.idx]}
        psb.w = dict(tok)
        psb.r = {}
        for b in reads:
            self._merge(b.r, tok)

    def dma(self, q, out_ap, in_ap, sb, load, **kw):
        if load:
            self._flush_for([sb])
            self._dma_now(q, out_ap, in_ap, sb, True, **kw)
        else:
            sb.pend += 1
            self.pending.append((q, out_ap, in_ap, sb, kw))
            while len(self.pending) > self.max_pending:
                self._emit_store(*self.pending.pop(0))

    def _emit_store(self, q, out_ap, in_ap, sb, kw):
        sb.pend -= 1
        self._dma_now(q, out_ap, in_ap, sb, False, **kw)

    def _dma_now(self, q, out_ap, in_ap, sb, load, **kw):
        E = self.E[q]
        toks = {}
        self._merge(toks, sb.w)
        if load:
            self._merge(toks, sb.r)
        self._wait(q, toks)
        if sb.dsem is None:
            sb.dsem = self.free_dsems.pop(0)
        kw.setdefault("allow_slow_non_contiguous", True)
        ins = E.obj.dma_start(out=out_ap, in_=in_ap, **kw)
        self.counts[sb.dsem] += 16
        ins.then_inc(self.sems[sb.dsem], 16)
        tok = {sb.dsem: self.counts[sb.dsem]}
        if load:
            sb.w = dict(tok)
            sb.r = {}
        else:
            self._merge(sb.r, tok)

    def load(self, q, sb, out_ap, in_ap, **kw):
        self.dma(q, out_ap, in_ap, sb, True, **kw)

    def store(self, q, sb, out_ap, in_ap, **kw):
        self.dma(q, out_ap, in_ap, sb, False, **kw)

    def barrier(self):
        self.flush_all()
        allt = {i: c for i, c in enumerate(self.counts) if c > 0}
        for e in self.E:
            E = self.E[e]
            for si, val in allt.items():
                if E.seen.get(si, 0) < val:
                    E.obj.wait_ge(self.sems[si], val)
                    E.seen[si] = val


def build(upto=99, debug=()):
    set_geo(GEO_FULL)
    nc = bass.Bass("TRN2", target_bir_lowering=False)
    dbg = set(debug)

    def din(name, shape, dt=F32):
        return nc.dram_tensor(name, list(shape), dt, kind="ExternalInput").ap()

    def dout(name, shape, dt=F32):
        return nc.dram_tensor(name, list(shape), dt, kind="ExternalOutput").ap()

    def dscr(name, shape, dt=BF16):
        return nc.dram_tensor(name, list(shape), dt, kind=("ExternalOutput" if name in dbg else "Internal")).ap()

    xT = din("xT", [D, T])
    csil = din("csil", [128, 32])
    ada_w = din("ada_w", [2, D, 6 * D])
    adab = din("adab", [128, 2, 96])
    nmix = din("nmix", [128, 2, 16])
    nffn = din("nffn", [128, 2, 16])
    e_w_in = din("e_w_in", [D, 6144])
    e_w_out = din("e_w_out", [D, D])
    o_w_in = din("o_w_in", [D, 5120])
    o_w_out = din("o_w_out", [D, D])
    ffn_in = din("ffn_in", [2, D, 11264])
    ffn_out = din("ffn_out", [2, 5632, D])
    conv_a = din("conv_a", [128, 8, 3])
    qkn = din("qkn", [128, 2])
    rpbp = din("rpbp", [8, 15, 127])
    ckT = din("ckT", [8, 128, 512])
    cv = din("cv", [512, 1024])
    ffn_conv = din("ffn_conv", [128, 2, 44, 3])
    j64 = din("j64", [64, 64])
    colmask = din("colmask", [128, 64])
    conf_dw = din("conf_dw", [128, 8, 31])
    conf_v = din("conf_v", [128, 3, 8])
    hy_sw = din("hy_sw", [128, 24, 3])
    hy_sb = din("hy_sb", [128, 24])
    hy_w1 = din("hy_w1", [33, 64])
    hy_w2 = din("hy_w2", [64, 64])
    hy_w3 = din("hy_w3", [64, 2048])
    hy_v = din("hy_v", [64, 4])
    hy_bias = din("hy_bias", [128, 8])
    absdel = din("absdel", [128, 1024])
    identf = din("identf", [128, 128])
    HYC = {}
    for L_ in (256, 4096):
        nbk = min(512, L_)
        HYC[L_] = dict(
            zT=din("zT%d" % L_, [33, L_]), negt=din("negt%d" % L_, [128, L_ // 128]),
            fwc=din("fwc%d" % L_, [L_ // 128, 128, L_ // 128, 128], BF16), fws=din("fws%d" % L_, [L_ // 128, 128, L_ // 128, 128], BF16),
            ivc=din("ivc%d" % L_, [L_ // nbk, 128, L_ // 128, nbk], BF16), ivs=din("ivs%d" % L_, [L_ // nbk, 128, L_ // 128, nbk], BF16),
            HS=dscr("HS%d" % L_, [2, L_, 1024], F32))
    qsel = din("qsel", [128, 4])
    ivcq = din("ivcq", [2, 128, 32, 512], BF16)
    ivsq = din("ivsq", [2, 128, 32, 512], BF16)
    ivh = din("ivh", [2, 128, 32, 2], BF16)
    hmask = din("hmask", [128, 2])
    yT = dout("yT", [D, T]) if upto < 8 else None
    yQ = dout("yQ", [D, GEO_TAIL["T"]]) if upto >= 8 else None
    kT_out = dout("kT_out", [1024, 512])
    v_out = dout("v_out", [512, 1024])
    xA = dscr("xA", [D, T], F32)
    xB = dscr("xB", [D, T], F32)
    PJ = dscr("PJ", [11264, TP])
    QN = dscr("QN", [1024, TP])
    KN = dscr("KN", [1024, TP])
    VT = dscr("VT", [T, 1024])
    MI = dscr("MI", [D, TP])
    FF = dscr("FF", [5632, TP])
    CV = dscr("CV", [1024, TP])
    PJ2 = dscr("PJ2", [11264, TP_TAIL])
    MI2 = dscr("MI2", [D, TP_TAIL])
    CV2 = dscr("CV2", [1024, TP_C])
    FF2 = dscr("FF2", [5632, TP_TAIL])
    xQ = dscr("xQ", [D, GEO_TAIL["T"]], F32)
    xA2 = dscr("xA2", [D, GEO_TAIL["T"]], F32)
    UU = dscr("UU", [1024, TP])
    X0 = dscr("X0", [1024, TP])
    YD = dscr("YD", [8, 2, 128, 32, 128])

    es = ExitStack()
    with es:
        fw = FW(nc, es)
        with fw.scope():
            ones_bf = fw.sb("ones_bf", [128, 128], BF16)
            eps_t = fw.sb("eps_t", [128, 1], F32)
            modA1 = [fw.sb("modA1_%d" % l, [128, 16, 2], F32) for l in range(2)]
            modB1 = [fw.sb("modB1_%d" % l, [128, 16, 2], F32) for l in range(2)]
            modG1 = [fw.sb("modG1_%d" % l, [128, 16, 2], F32) for l in range(2)]
            modA2 = [fw.sb("modA2_%d" % l, [128, 16, 2], F32) for l in range(2)]
            modB2 = [fw.sb("modB2_%d" % l, [128, 16, 2], F32) for l in range(2)]
            modG2 = [fw.sb("modG2_%d" % l, [128, 16, 2], F32) for l in range(2)]
            fw.op("dve", lambda: nc.vector.memset(ones_bf[:], 1.0), w=[ones_bf])
            fw.op("dve", lambda: nc.vector.memset(eps_t[:], EPS), w=[eps_t])

            with fw.scope():
                cs = fw.sb("cs", [128, 32], F32)
                csb = fw.sb("csb", [128, 16, 2], BF16)
                adab_t = fw.sb("adab_t", [128, 2, 96], F32)
                nmix_t = fw.sb("nmix_t", [128, 2, 16], F32)
                nffn_t = fw.sb("nffn_t", [128, 2, 16], F32)
                mod = fw.sb("mod", [128, 96, 2], F32)
                tmp = fw.sb("tmpm", [128, 16, 2], F32)
                wts = [fw.sb("adw%d" % i, [128, 16, 512], BF16) for i in range(2)]
                fw.load("sp", cs, cs[:], csil)
                fw.load("sp", adab_t, adab_t[:], adab)
                fw.load("sp", nmix_t, nmix_t[:], nmix)
                fw.load("sp", nffn_t, nffn_t[:], nffn)
                fw.op("act", lambda: nc.scalar.activation(out=csb[:].rearrange("p a b -> p (a b)"), in_=cs[:], func=AF.Silu), r=[cs], w=[csb])
                wi = 0
                for l in range(2):
                    psb = fw.psum()
                    wv = ada_w[l].rearrange("(kc p) n -> p kc n", p=128)
                    for wt in range(24):
                        wb = wts[wi % 2]
                        wi += 1
                        fw.load("pool", wb, wb[:], wv[:, :, wt * 512:(wt + 1) * 512])
                        for c in range(4):
                            fc = wt * 4 + c
                            fw.mm(psb, psb[:, fc * 2:fc * 2 + 2],
                                  [(wb[:, kc, c * 128:(c + 1) * 128], csb[:, kc, :]) for kc in range(16)], [wb, csb])
                    fw.op("dve", lambda: nc.vector.tensor_tensor(out=mod[:], in0=psb[:, 0:192].rearrange("p (a b) -> p a b", b=2),
                                                                 in1=adab_t[:, l, :].unsqueeze(2).to_broadcast([128, 96, 2]), op=ALU.add),
                          r=[psb, adab_t], w=[mod])
                    for (A, Bm, G, nt, base) in ((modA1[l], modB1[l], modG1[l], nmix_t, 0), (modA2[l], modB2[l], modG2[l], nffn_t, 48)):
                        fw.op("dve", lambda: nc.vector.tensor_scalar(out=tmp[:], in0=mod[:, base + 16:base + 32, :], scalar1=1.0, scalar2=None, op0=ALU.add), r=[mod], w=[tmp])
                        fw.op("dve", lambda: nc.vector.tensor_tensor(out=A[:], in0=tmp[:], in1=nt[:, l, :].unsqueeze(2).to_broadcast([128, 16, 2]), op=ALU.mult), r=[tmp, nt], w=[A])
                        fw.op("dve", lambda: nc.vector.tensor_copy(out=Bm[:], in_=mod[:, base:base + 16, :]), r=[mod], w=[Bm])
                        fw.op("dve", lambda: nc.vector.tensor_copy(out=G[:], in_=mod[:, base + 32:base + 48, :]), r=[mod], w=[G])

            def norm_phase(x_src, A, Bm, hT):
                with fw.scope():
                    xts = [fw.sb("nx%d" % i, [128, 16, 256], F32) for i in range(2)]
                    sq = fw.sb("nsq", [128, 16, 256], BF16)
                    rs = fw.sb("nrs", [128, 256], F32)
                    xv = x_src.rearrange("(kc p) t -> p kc t", p=128)
                    subt = [(t0_ + s_, min(256, n_ - s_)) for (t0_, n_) in TILES for s_ in range(0, n_, 256)]
                    for i, (t0, nn) in enumerate(subt):
                        v = tile_v(t0)
                        xt = xts[i % 2]
                        fw.load("sp", xt, xt[:, :, 0:nn], xv[:, :, t0:t0 + nn])
                        fw.op("act", lambda: nc.scalar.activation(out=sq[:, :, 0:nn], in_=xt[:, :, 0:nn], func=AF.Square), r=[xt], w=[sq])
                        psb = fw.psum()
                        fw.mm(psb, psb[:, 0:nn], [(ones_bf[:], sq[:, kc, 0:nn]) for kc in range(16)], [ones_bf, sq])
                        fw.op("act", lambda: nc.scalar.activation(out=rs[:, 0:nn], in_=psb[:, 0:nn], func=AF.Sqrt, bias=eps_t[:, 0:1], scale=1.0 / D), r=[psb, eps_t], w=[rs])
                        fw.op("dve", lambda: nc.vector.reciprocal(out=rs[:, 0:nn], in_=rs[:, 0:nn]), r=[rs], w=[rs])
                        fw.op("dve", lambda: nc.vector.tensor_tensor(out=xt[:, :, 0:nn], in0=xt[:, :, 0:nn], in1=rs[:, 0:nn].unsqueeze(1).to_broadcast([128, 16, nn]), op=ALU.mult), r=[xt, rs], w=[xt])
                        fw.op("dve", lambda: nc.vector.tensor_tensor(out=xt[:, :, 0:nn], in0=xt[:, :, 0:nn], in1=A[:, :, v].unsqueeze(2).to_broadcast([128, 16, nn]), op=ALU.mult), r=[xt, A], w=[xt])
                        fw.op("pool", lambda: nc.gpsimd.tensor_tensor(out=hT[:, :, t0:t0 + nn], in0=xt[:, :, 0:nn], in1=Bm[:, :, v].unsqueeze(2).to_broadcast([128, 16, nn]), op=ALU.add), r=[xt, Bm], w=[hT])

            evac_i = [0]

            def evac(out_ap, in_ap, r, w, e=None):
                evac_i[0] += 1
                if (evac_i[0] % 2) if e is None else (e % 2):
                    fw.op("act", lambda: nc.scalar.copy(out=out_ap, in_=in_ap), r=r, w=w)
                else:
                    fw.op("dve", lambda: nc.vector.tensor_copy(out=out_ap, in_=in_ap), r=r, w=w)

            def proj_rows(actT, KC, W, col0, ncols, dst, dst_row0, wts, rows):
                wv = W.rearrange("(kc p) n -> p kc n", p=128)
                nw = ncols // 512
                fw.load("pool", wts[0], wts[0][:, 0:KC, :], wv[:, :, col0:col0 + 512])
                ri = 0
                for wt in range(nw):
                    wb = wts[wt % 2]
                    if wt + 1 < nw:
                        nb = wts[(wt + 1) % 2]
                        fw.load("pool", nb, nb[:, 0:KC, :], wv[:, :, col0 + (wt + 1) * 512:col0 + (wt + 2) * 512])
                    for c in range(4):
                        row = rows[ri % len(rows)]
                        ri += 1
                        for (t0, n) in TILES:
                            psb = fw.psum()
                            fw.mm(psb, psb[:, 0:n], [(wb[:, kc, c * 128:(c + 1) * 128], actT[:, kc, t0:t0 + n]) for kc in range(KC)], [wb, actT])
                            p0 = ppos(t0)
                            evac(row[:, p0:p0 + n], psb[:, 0:n], [psb], [row], ri)
                        r0 = dst_row0 + (wt * 4 + c) * 128
                        fw.store("sp", row, dst[r0:r0 + 128, :], row[:])

            def make_rows(n=3):
                rows = [fw.sb("prow%d" % i, [128, TP], BF16) for i in range(n)]
                for rr in rows:
                    fw.op("pool", lambda: nc.gpsimd.memset(rr[:], 0.0), w=[rr])
                return rows

            def proj_resid(actT, KC, W, x_src, x_dst, G, wts):
                wv = W.rearrange("(kc p) n -> p kc n", p=128)
                xts = [fw.sb("rx%d" % i, [128, 512], F32) for i in range(3)]
                xos = [fw.sb("ro%d" % i, [128, 512], F32) for i in range(3)]
                nw = D // 512
                fw.load("pool", wts[0], wts[0][:, 0:KC, :], wv[:, :, 0:512])
                k = 0
                for wt in range(nw):
                    wb = wts[wt % 2]
                    if wt + 1 < nw:
                        nb = wts[(wt + 1) % 2]
                        fw.load("pool", nb, nb[:, 0:KC, :], wv[:, :, (wt + 1) * 512:(wt + 2) * 512])
                    for c in range(4):
                        oc = wt * 4 + c
                        for (t0, n) in TILES:
                            v = tile_v(t0)
                            xt = xts[k % 3]
                            xo = xos[k % 3]
                            k += 1
                            fw.load("sp", xt, xt[:, 0:n], x_src[oc * 128:(oc + 1) * 128, t0:t0 + n])
                            psb = fw.psum()
                            fw.mm(psb, psb[:, 0:n], [(wb[:, kc, c * 128:(c + 1) * 128], actT[:, kc, t0:t0 + n]) for kc in range(KC)], [wb, actT])
                            fw.op("dve", lambda: nc.vector.scalar_tensor_tensor(out=xo[:, 0:n], in0=psb[:, 0:n], scalar=G[:, oc, v:v + 1], in1=xt[:, 0:n], op0=ALU.mult, op1=ALU.add),
                                  r=[psb, G, xt], w=[xo])
                            fw.store("sp", xo, x_dst[oc * 128:(oc + 1) * 128, t0:t0 + n], xo[:, 0:n])

            if upto >= 1:
                with fw.scope():
                    hT = fw.sb("hT", [128, 16, T], BF16)
                    norm_phase(xT, modA1[0], modB1[0], hT)
                    with fw.scope():
                      if upto >= 1.3:
                        wts = [fw.sb("wt%d" % i, [128, 16, 512], BF16) for i in range(2)]
                        rows = make_rows(2)
                        proj_rows(hT, 16, e_w_in, 0, 5120, PJ, 0, wts, rows)
                    with fw.scope():
                      if upto >= 1.6:
                        wts = [fw.sb("wt%d" % i, [128, 16, 512], BF16) for i in range(2)]
                        wv = e_w_in.rearrange("(kc p) n -> p kc n", p=128)
                        vst = [fw.sb("vst%d" % i, [128, 512], BF16) for i in range(2)]
                        vst32 = [fw.sb("vstf%d" % i, [128, 512], F32) for i in range(2)]
                        k = 0
                        for half in range(2):
                            wb = wts[half]
                            fw.load("pool", wb, wb[:], wv[:, :, 5120 + half * 512:5120 + (half + 1) * 512])
                            for tb in range(T // 128):
                                psb = fw.psum()
                                fw.mm(psb, psb[:, :], [(hT[:, kc, tb * 128:(tb + 1) * 128], wb[:, kc, :]) for kc in range(16)], [wb, hT])
                                vs = vst[k % 2]
                                k += 1
                                if tb < 4:
                                    vf = vst32[k % 2]
                                    evac(vf[:], psb[:, :], [psb], [vf])
                                    fw.store("sp", vf, v_out[tb * 128:(tb + 1) * 128, half * 512:(half + 1) * 512], vf[:])
                                    fw.op("pool", lambda: nc.gpsimd.tensor_copy(out=vs[:], in_=vf[:]), r=[vf], w=[vs])
                                else:
                                    evac(vs[:], psb[:, :], [psb], [vs])
                                fw.store("sp", vs, VT[tb * 128:(tb + 1) * 128, half * 512:(half + 1) * 512], vs[:])

            if upto >= 2.0:
                with fw.scope():
                    cw = fw.sb("cw", [128, 8, 3], F32)
                    qk_g = fw.sb("qk_g", [128, 2], F32)
                    fw.load("sp", cw, cw[:], conv_a)
                    fw.load("sp", qk_g, qk_g[:], qkn)
                    rb = [[fw.sb("scr%d_%d" % (i, j), [128, TP], BF16) for j in range(3)] for i in range(2)]
                    pp = fw.sb("scp", [128, TP], F32)
                    oo = fw.sb("sco", [128, TP], F32)
                    ob = [fw.sb("scob%d" % i, [128, TP], BF16) for i in range(2)]
                    for j in range(8):
                        ab, ac, ax = rb[j % 2]
                        fw.load("sp", ab, ab[:], PJ[j * 128:(j + 1) * 128, :])
                        fw.load("sp", ac, ac[:], PJ[(8 + j) * 128:(9 + j) * 128, :])
                        fw.load("sp", ax, ax[:], PJ[(16 + j) * 128:(17 + j) * 128, :])
                        fw.op("dve", lambda: nc.vector.tensor_tensor(out=pp[:], in0=ac[:], in1=ax[:], op=ALU.mult), r=[ac, ax], w=[pp])
                        fw.op("dve", lambda: nc.vector.tensor_scalar(out=oo[:, 1:TP - 1], in0=pp[:, 0:TP - 2], scalar1=cw[:, j, 0:1], scalar2=None, op0=ALU.mult), r=[pp, cw], w=[oo])
                        fw.op("dve", lambda: nc.vector.scalar_tensor_tensor(out=oo[:, 1:TP - 1], in0=pp[:, 1:TP - 1], scalar=cw[:, j, 1:2], in1=oo[:, 1:TP - 1], op0=ALU.mult, op1=ALU.add), r=[pp, cw, oo], w=[oo])
                        fw.op("dve", lambda: nc.vector.scalar_tensor_tensor(out=oo[:, 1:TP - 1], in0=pp[:, 2:TP], scalar=cw[:, j, 2:3], in1=oo[:, 1:TP - 1], op0=ALU.mult, op1=ALU.add), r=[pp, cw, oo], w=[oo])
                        o = ob[j % 2]
                        fw.op("dve", lambda: nc.vector.tensor_tensor(out=o[:, 1:TP - 1], in0=oo[:, 1:TP - 1], in1=ab[:, 1:TP - 1], op=ALU.mult), r=[oo, ab], w=[o])
                        fw.store("sp", o, MI[j * 128:(j + 1) * 128, 1:TP - 1], o[:, 1:TP - 1])
                    sq = fw.sb("qsq", [128, TP], BF16)
                    rs = fw.sb("qrs", [128, TP], F32)
                    kf = fw.sb("kf32", [128, 544], F32)
                    i = 0
                    for which in range(2):
                        for h in range(8):
                            row = rb[i % 2][0]
                            o = ob[i % 2]
                            i += 1
                            fw.load("sp", row, row[:], PJ[(24 + which * 8 + h) * 128:(25 + which * 8 + h) * 128, :])
                            fw.op("act", lambda: nc.scalar.activation(out=sq[:], in_=row[:], func=AF.Square), r=[row], w=[sq])
                            for c0 in range(0, TP, 512):
                                n = min(512, TP - c0)
                                psb = fw.psum()
                                fw.mm(psb, psb[:, 0:n], [(ones_bf[:], sq[:, c0:c0 + n])], [ones_bf, sq])
                                fw.op("act", lambda: nc.scalar.activation(out=rs[:, c0:c0 + n], in_=psb[:, 0:n], func=AF.Sqrt, bias=eps_t[:, 0:1], scale=1.0 / 128), r=[psb, eps_t], w=[rs])
                            fw.op("dve", lambda: nc.vector.reciprocal(out=rs[:], in_=rs[:]), r=[rs], w=[rs])
                            fw.op("dve", lambda: nc.vector.scalar_tensor_tensor(out=o[:], in0=row[:], scalar=qk_g[:, which:which + 1], in1=rs[:], op0=ALU.mult, op1=ALU.mult), r=[row, qk_g, rs], w=[o])
                            dst = QN if which == 0 else KN
                            fw.store("sp", o, dst[h * 128:(h + 1) * 128, :], o[:])
                            if which == 1:
                                fw.op("dve", lambda: nc.vector.scalar_tensor_tensor(out=kf[:], in0=row[:, 0:544], scalar=qk_g[:, 1:2], in1=rs[:, 0:544], op0=ALU.mult, op1=ALU.mult), r=[row, qk_g, rs], w=[kf])
                                fw.store("sp", kf, kT_out[h * 128:(h + 1) * 128, 0:256], kf[:, 16:272])
                                fw.store("sp", kf, kT_out[h * 128:(h + 1) * 128, 256:512], kf[:, 288:544])


            SP0 = ppos(512)
            SCALE = 128.0 ** -0.5

            def attention_phase():
                with fw.scope():
                    vall = fw.sb("vall", [128, 36, 1024], BF16)
                    ck = fw.sb("ck", [128, 8, 512], BF16)
                    cvb = fw.sb("cvb", [128, 4, 1024], BF16)
                    fw.load("sp", vall, vall[:], VT.rearrange("(b p) c -> p b c", p=128))
                    fw.load("pool", ck, ck[:], ckT.rearrange("h d k -> d h k"))
                    fw.load("pool", cvb, cvb[:], cv.rearrange("(b p) c -> p b c", p=128))
                    MG = fw.sb("MG", [128, 8, 5, 128], BF16)
                    ME = [fw.sb("ME%d" % i, [128, 8, 4, 128], BF16) for i in range(4)]
                    with fw.scope():
                        tz = fw.sb("tz", [128, 8, 15, 64], BF16)
                        jt = fw.sb("jt", [64, 64], F32)
                        cm = fw.sb("cm", [128, 64], F32)
                        fw.load("sp", jt, jt[:], j64)
                        fw.load("sp", cm, cm[:], colmask)
                        hh = [fw.sb("hh%d" % i, [64, 15, 128], F32) for i in range(2)]
                        te = fw.sb("te", [128, 15, 64], F32)
                        for h in range(8):
                            hb = hh[h % 2]
                            src = AP(rpbp.tensor, h * 15 * 127, [[1, 64], [127, 15], [1, 64]])
                            fw.load("sp", hb, hb[:, :, 0:64], src)
                            fw.load("sp", hb, hb[:, :, 64:128], src)
                            p1 = fw.psum()
                            p2 = fw.psum()
                            for a in range(15):
                                pb = p1 if a < 8 else p2
                                fw.mm(pb, pb[:, (a % 8) * 64:(a % 8) * 64 + 64], [(hb[:, a, :], jt[:])], [hb, jt])
                            fw.op("act", lambda: nc.scalar.activation(out=te[:, 0:8, :], in_=p1[:, :].rearrange("p (a k) -> p a k", k=64), func=AF.Exp), r=[p1], w=[te])
                            fw.op("act", lambda: nc.scalar.activation(out=te[:, 8:15, :], in_=p2[:, 0:448].rearrange("p (a k) -> p a k", k=64), func=AF.Exp), r=[p2], w=[te])
                            fw.op("dve", lambda: nc.vector.tensor_tensor(out=tz[:, h, :, :], in0=te[:], in1=cm[:].unsqueeze(1).to_broadcast([128, 15, 64]), op=ALU.mult), r=[te, cm], w=[tz])
                        for M in [MG] + ME:
                            fw.op("pool", lambda: nc.gpsimd.memset(M[:], 0.0), w=[M])
                        defs = [(MG, 5, -4, True)] + [(ME[0], 4, 0, False), (ME[1], 4, -2, False), (ME[2], 4, -4, False), (ME[3], 4, -6, False)]
                        for (M, nb, off, chk) in defs:
                            for b in range(nb):
                                for kr in range(2):
                                    for qr in range(2):
                                        dr = 2 * b + kr + off - qr
                                        if chk and not (-4 <= dr <= 3):
                                            continue
                                        a = dr + 7
                                        assert 0 <= a <= 14
                                        fw.op("dve", lambda: nc.vector.tensor_copy(out=M[kr * 64:(kr + 1) * 64, :, b, qr * 64:(qr + 1) * 64], in_=tz[kr * 64:(kr + 1) * 64, :, a, :]), r=[tz], w=[M])
                    qrows = [fw.sb("aq%d" % i, [128, TP], BF16) for i in range(2)]
                    krows = [fw.sb("ak%d" % i, [128, TP], BF16) for i in range(2)]
                    yrows = [fw.sb("ay%d" % i, [128, TP], BF16) for i in range(2)]
                    for yr in yrows:
                        fw.op("pool", lambda: nc.gpsimd.memset(yr[:], 0.0), w=[yr])
                    els = [fw.sb("el%d" % i, [128, 640], F32) for i in range(2)]
                    ets = [fw.sb("et%d" % i, [128, 640], BF16) for i in range(2)]
                    ecs = [fw.sb("ec%d" % i, [128, 512], BF16) for i in range(2)]
                    rcs = [fw.sb("rc%d" % i, [128, 256], F32) for i in range(2)]
                    it = 0
                    import os
                    ATT = int(os.environ.get("ATT_STAGE", "9"))
                    for h in range(8 if ATT >= 1 else 0):
                        qh = qrows[h % 2]
                        kh = krows[h % 2]
                        yr = yrows[h % 2]
                        fw.load("sp", qh, qh[:], QN[h * 128:(h + 1) * 128, :])
                        fw.load("sp", kh, kh[:], KN[h * 128:(h + 1) * 128, :])
                        hs = slice(h * 128, (h + 1) * 128)
                        for s_ in range(2):
                            p0 = ppos(s_ * 256)
                            et = ets[it % 2]
                            rc = rcs[it % 2]
                            it += 1
                            pS = fw.psum()
                            for kb in range(2):
                                fw.mm(pS, pS[:, kb * 256:(kb + 1) * 256], [(kh[:, p0 + kb * 128:p0 + (kb + 1) * 128], qh[:, p0:p0 + 256])], [kh, qh])
                            fw.op("act", lambda: nc.scalar.activation(out=et[:, 0:512], in_=pS[:, :], func=AF.Exp, scale=SCALE), r=[pS], w=[et])
                            pO = fw.psum()
                            fw.mm(pO, pO[:, 0:256], [(vall[:, s_ * 2 + kb, hs], et[:, kb * 256:(kb + 1) * 256]) for kb in range(2)], [vall, et])
                            fw.mm(pO, pO[:, 256:512], [(ones_bf[:], et[:, kb * 256:(kb + 1) * 256]) for kb in range(2)], [ones_bf, et])
                            fw.op("dve", lambda: nc.vector.reciprocal(out=rc[:, 0:256], in_=pO[:, 256:512]), r=[pO], w=[rc])
                            fw.op("dve", lambda: nc.vector.tensor_tensor(out=yr[:, p0:p0 + 256], in0=pO[:, 0:256], in1=rc[:, 0:256], op=ALU.mult), r=[pO, rc], w=[yr])
                        def stageA(m):
                            if m < 2:
                                lb = [0, 1, 2, 3]
                                M = ME[m]
                            elif m >= 30:
                                lb = [28, 29, 30, 31]
                                M = ME[2 + m - 30]
                            else:
                                lb = [m - 2, m - 1, m, m + 1, m + 2]
                                M = MG
                            nb = len(lb)
                            st = dict(lb=lb, nb=nb, el=els[it_[0] % 2], et=ets[it_[0] % 2], ec=ecs[it_[0] % 2], rc=rcs[it_[0] % 2])
                            it_[0] += 1
                            el, et, ec = st["el"], st["et"], st["ec"]
                            q0 = SP0 + m * 128
                            st["q0"] = q0
                            qt = qh[:, q0:q0 + 128]
                            pL = fw.psum()
                            pC = fw.psum()
                            pO = fw.psum()
                            st["pO"] = pO
                            for i in range(4):
                                fw.mm(pL, pL[:, i * 128:(i + 1) * 128], [(kh[:, SP0 + lb[i] * 128:SP0 + (lb[i] + 1) * 128], qt)], [kh, qh])
                            if nb == 5:
                                fw.mm(pO, pO[:, 256:384], [(kh[:, SP0 + lb[4] * 128:SP0 + (lb[4] + 1) * 128], qt)], [kh, qh])
                            for cb in range(4):
                                fw.mm(pC, pC[:, cb * 128:(cb + 1) * 128], [(ck[:, h, cb * 128:(cb + 1) * 128], qt)], [ck, qh])
                            fw.op("act", lambda: nc.scalar.activation(out=el[:, 0:512], in_=pL[:, :], func=AF.Exp, scale=SCALE), r=[pL], w=[el])
                            if nb == 5:
                                fw.op("act", lambda: nc.scalar.activation(out=el[:, 512:640], in_=pO[:, 256:384], func=AF.Exp, scale=SCALE), r=[pO], w=[el])
                            fw.op("act", lambda: nc.scalar.activation(out=ec[:], in_=pC[:, :], func=AF.Exp, scale=SCALE), r=[pC], w=[ec])
                            fw.op("dve", lambda: nc.vector.tensor_tensor(out=et[:, 0:nb * 128], in0=el[:, 0:nb * 128], in1=M[:, h, :, :].rearrange("p b q -> p (b q)"), op=ALU.mult), r=[el, M], w=[et])
                            return st

                        def stageB(st):
                            lb, nb, et, ec, rc, pO, q0 = st["lb"], st["nb"], st["et"], st["ec"], st["rc"], st["pO"], st["q0"]
                            pairsO = [(vall[:, 4 + lb[i], hs], et[:, i * 128:(i + 1) * 128]) for i in range(nb)] + [(cvb[:, cb, hs], ec[:, cb * 128:(cb + 1) * 128]) for cb in range(4)]
                            pairsS = [(ones_bf[:], et[:, i * 128:(i + 1) * 128]) for i in range(nb)] + [(ones_bf[:], ec[:, cb * 128:(cb + 1) * 128]) for cb in range(4)]
                            fw.mm(pO, pO[:, 0:128], pairsO, [vall, cvb, et, ec])
                            fw.mm(pO, pO[:, 128:256], pairsS, [ones_bf, et, ec])
                            fw.op("dve", lambda: nc.vector.reciprocal(out=rc[:, 0:128], in_=pO[:, 128:256]), r=[pO], w=[rc])
                            fw.op("dve", lambda: nc.vector.tensor_tensor(out=yr[:, q0:q0 + 128], in0=pO[:, 0:128], in1=rc[:, 0:128], op=ALU.mult), r=[pO, rc], w=[yr])
                        it_ = [it]
                        ms_ = list([int(x) for x in os.environ['ATT_M'].split(',')] if 'ATT_M' in os.environ else range(32 if ATT >= 2 else 0))
                        prev_st = None
                        for m in ms_:
                            cur = stageA(m)
                            if prev_st is not None:
                                stageB(prev_st)
                            prev_st = cur
                        if prev_st is not None:
                            stageB(prev_st)
                        it = it_[0]
                        fw.store("sp", yr, MI[1024 + h * 128:1024 + (h + 1) * 128, :], yr[:])

            def load_act_from(src, actT):
                sv = src.rearrange("(kc p) t -> p kc t", p=128)
                for (s0, ln) in SEGS:
                    p0 = ppos(s0)
                    fw.load("sp", actT, actT[:, :, s0:s0 + ln], sv[:, :, p0:p0 + ln])

            def wout_phase(W, x_src, x_dst, G):
                with fw.scope():
                    mT = fw.sb("mT", [128, 16, T], BF16)
                    load_act_from(MI, mT)
                    wts = [fw.sb("wo%d" % i, [128, 16, 512], BF16) for i in range(2)]
                    proj_resid(mT, 16, W, x_src, x_dst, G, wts)

            def ffn_phase(l, x_src, x_dst, PJ=PJ, FF=FF, halo=None):
                with fw.scope():
                    hT = fw.sb("hTf", [128, 16, T], BF16)
                    norm_phase(x_src, modA2[l], modB2[l], hT)
                    with fw.scope():
                        wts = [fw.sb("wf%d" % i, [128, 16, 512], BF16) for i in range(2)]
                        rows = make_rows(2)
                        proj_rows(hT, 16, ffn_in[l], 0, 11264, PJ, 0, wts, rows)
                wq_scope = fw.scope()
                wq_scope.__enter__()
                whc = [fw.sb("whc%d" % i, [128, 44, 128], BF16) for i in range(8)]
                wv = ffn_out[l].rearrange("(kc p) n -> p kc n", p=128)

                def load_wc(half, c):
                    col = (half * 8 + c) * 128
                    for h2 in range(2):
                        fw.load("pool", whc[c], whc[c][:, h2 * 22:(h2 + 1) * 22, :], wv[:, h2 * 22:(h2 + 1) * 22, col:col + 128])
                for c_ in range(8):
                    load_wc(0, c_)
                with fw.scope():
                    fcw = fw.sb("fcw", [128, 44, 3], F32)
                    fw.load("sp", fcw, fcw[:], ffn_conv[:, l, :, :])
                    ar = [fw.sb("fa%d" % i, [128, TP], BF16) for i in range(2)]
                    gr = [fw.sb("fg%d" % i, [128, TP], BF16) for i in range(2)]
                    oo = [fw.sb("fo%d" % i, [128, TP], F32) for i in range(2)]
                    ge = [fw.sb("fe%d" % i, [128, TP], BF16) for i in range(2)]
                    ob = [fw.sb("fb%d" % i, [128, TP], BF16) for i in range(2)]
                    for o in ob:
                        fw.op("pool", lambda: nc.gpsimd.memset(o[:], 0.0), w=[o])
                    for j in range(44):
                        a = ar[j % 2]
                        g = gr[j % 2]
                        o32 = oo[j % 2]
                        gg = ge[j % 2]
                        o = ob[j % 2]
                        fw.load("sp", a, a[:], PJ[j * 128:(j + 1) * 128, :])
                        fw.load("sp", g, g[:], PJ[5632 + j * 128:5632 + (j + 1) * 128, :])
                        if halo is not None:
                            for hi_, hp_ in enumerate((ppos(512), ppos(1537))):
                                fw.op("dve", lambda: nc.vector.tensor_scalar(out=a[:, hp_:hp_ + 1], in0=a[:, hp_:hp_ + 1], scalar1=halo[:, hi_:hi_ + 1], scalar2=None, op0=ALU.mult), r=[a, halo], w=[a])
                        fw.op("dve", lambda: nc.vector.tensor_scalar(out=o32[:, 1:TP - 1], in0=a[:, 0:TP - 2], scalar1=fcw[:, j, 0:1], scalar2=None, op0=ALU.mult), r=[a, fcw], w=[o32])
                        fw.op("dve", lambda: nc.vector.scalar_tensor_tensor(out=o32[:, 1:TP - 1], in0=a[:, 1:TP - 1], scalar=fcw[:, j, 1:2], in1=o32[:, 1:TP - 1], op0=ALU.mult, op1=ALU.add), r=[a, fcw, o32], w=[o32])
                        fw.op("dve", lambda: nc.vector.scalar_tensor_tensor(out=o32[:, 1:TP - 1], in0=a[:, 2:TP], scalar=fcw[:, j, 2:3], in1=o32[:, 1:TP - 1], op0=ALU.mult, op1=ALU.add), r=[a, fcw, o32], w=[o32])
                        fw.op("act", lambda: nc.scalar.activation(out=gg[:, 1:TP - 1], in_=o32[:, 1:TP - 1], func=AF.Gelu_apprx_tanh), r=[o32], w=[gg])
                        fw.op("pool", lambda: nc.gpsimd.tensor_tensor(out=o[:, 1:TP - 1], in0=gg[:, 1:TP - 1], in1=g[:, 1:TP - 1], op=ALU.mult), r=[gg, g], w=[o])
                        fw.store("sp", o, FF[j * 128:(j + 1) * 128, :], o[:])
                with fw.scope():
                    G = modG2[l]
                    fts = [fw.sb("ft%d" % i, [128, 44, 512], BF16) for i in range(2)]
                    xts = [fw.sb("fx%d" % i, [128, 512], F32) for i in range(3)]
                    xos = [fw.sb("fxo%d" % i, [128, 512], F32) for i in range(3)]
                    fv = FF.rearrange("(kc p) t -> p kc t", p=128)
                    k = 0
                    ti = 0
                    seq_ = [(half, tix, t0, n) for half in range(2) for tix, (t0, n) in enumerate(TILES)]

                    def load_ft(i_):
                        _, _, t0_, n_ = seq_[i_]
                        fw.load("sp", fts[i_ % 2], fts[i_ % 2][:, :, 0:n_], fv[:, :, ppos(t0_):ppos(t0_) + n_])
                    load_ft(0)
                    for si_, (half, tix, t0, n) in enumerate(seq_):
                        if True:
                            v = tile_v(t0)
                            ft = fts[si_ % 2]
                            if si_ + 1 < len(seq_):
                                load_ft(si_ + 1)
                            for c in range(8):
                                oc = half * 8 + c
                                wh = whc[c]
                                xt = xts[k % 3]
                                xo = xos[k % 3]
                                k += 1
                                fw.load("sp", xt, xt[:, 0:n], x_src[oc * 128:(oc + 1) * 128, t0:t0 + n])
                                psb = fw.psum()
                                fw.mm(psb, psb[:, 0:n], [(wh[:, kc, :], ft[:, kc, 0:n]) for kc in range(44)], [wh, ft])
                                fw.op("dve", lambda: nc.vector.scalar_tensor_tensor(out=xo[:, 0:n], in0=psb[:, 0:n], scalar=G[:, oc, v:v + 1], in1=xt[:, 0:n], op0=ALU.mult, op1=ALU.add),
                                      r=[psb, G, xt], w=[xo])
                                fw.store("sp", xo, x_dst[oc * 128:(oc + 1) * 128, t0:t0 + n], xo[:, 0:n])
                                if half == 0 and tix == len(TILES) - 1:
                                    load_wc(1, c)
                wq_scope.__exit__(None, None, None)

            def conformer_phase():
                with fw.scope():
                    cdw = fw.sb("cdw", [128, 8, 31], F32)
                    cvv = fw.sb("cvv", [128, 3, 8], F32)
                    fw.load("sp", cdw, cdw[:], conf_dw)
                    fw.load("sp", cvv, cvv[:], conf_v)
                    qs = fw.sb("cqs", [128, 4], F32)
                    fw.load("sp", qs, qs[:], qsel)
                    carf = [fw.sb("carf%d" % i, [128, TP], BF16) for i in range(2)]
                    cgrf = [fw.sb("cgrf%d" % i, [128, TP], BF16) for i in range(2)]
                    car = [fw.sb("car%d" % i, [128, TP_C], BF16) for i in range(2)]
                    cgr = [fw.sb("cgr%d" % i, [128, TP_C], BF16) for i in range(2)]
                    sg = fw.sb("csg", [128, TP_C], F32)
                    u32 = fw.sb("cu32", [128, TP_C], F32)
                    oA = fw.sb("coA", [128, TP_C], F32)
                    cvo = [fw.sb("cvo%d" % i, [128, TP_C], BF16) for i in range(2)]
                    for o in cvo + car + cgr:
                        fw.op("pool", lambda: nc.gpsimd.memset(o[:], 0.0), w=[o])
                    W = TP_C - 30
                    for j in range(8):
                        caf = carf[j % 2]
                        cgf = cgrf[j % 2]
                        ca = car[j % 2]
                        cg = cgr[j % 2]
                        o = cvo[j % 2]
                        fw.load("sp", caf, caf[:], PJ[j * 128:(j + 1) * 128, :])
                        fw.load("sp", cgf, cgf[:], PJ[(8 + j) * 128:(9 + j) * 128, :])
                        for (full, qq) in ((caf, ca), (cgf, cg)):
                            fw.op("pool", lambda: nc.gpsimd.tensor_copy(out=qq[:, 0:560], in_=full[:, 0:560]), r=[full], w=[qq])
                            fw.op("dve", lambda: nc.vector.tensor_scalar(out=qq[:, 560:1616], in0=full[:, 544:544 + 1056], scalar1=qs[:, 0:1], scalar2=None, op0=ALU.mult), r=[full, qs], w=[qq])
                            for q in range(1, 4):
                                fw.op("dve", lambda: nc.vector.scalar_tensor_tensor(out=qq[:, 560:1616], in0=full[:, 544 + q * 1024:544 + q * 1024 + 1056], scalar=qs[:, q:q + 1], in1=qq[:, 560:1616], op0=ALU.mult, op1=ALU.add), r=[full, qs, qq], w=[qq])
                        fw.op("act", lambda: nc.scalar.activation(out=sg[:], in_=cg[:], func=AF.Sigmoid), r=[cg], w=[sg])
                        fw.op("dve", lambda: nc.vector.tensor_tensor(out=u32[:], in0=ca[:], in1=sg[:], op=ALU.mult), r=[ca, sg], w=[u32])
                        fw.op("dve", lambda: nc.vector.tensor_scalar(out=oA[:, 15:15 + W], in0=u32[:, 0:W], scalar1=cdw[:, j, 0:1], scalar2=cvv[:, 0, j:j + 1], op0=ALU.mult, op1=ALU.add), r=[u32, cdw, cvv], w=[oA])
                        for k in range(1, 31):
                            fw.op("dve", lambda: nc.vector.scalar_tensor_tensor(out=oA[:, 15:15 + W], in0=u32[:, k:k + W], scalar=cdw[:, j, k:k + 1], in1=oA[:, 15:15 + W], op0=ALU.mult, op1=ALU.add), r=[u32, cdw, oA], w=[oA])
                        fw.op("act", lambda: nc.scalar.copy(out=o[:, 15:15 + W], in_=oA[:, 15:15 + W]), r=[oA], w=[o])
                        fw.store("sp", o, CV2[j * 128:(j + 1) * 128, :], o[:])
                with fw.scope():
                    cvv = fw.sb("cvv2", [128, 3, 8], F32)
                    fw.load("sp", cvv, cvv[:], conf_v)
                    cvts = [fw.sb("cvt%d" % i, [128, 8, 512], BF16) for i in range(2)]
                    sq = fw.sb("lsq", [128, 8, 512], BF16)
                    mean = fw.sb("lmean", [128, 512], F32)
                    msq = fw.sb("lmsq", [128, 512], F32)
                    var = fw.sb("lvar", [128, 512], F32)
                    xc = fw.sb("lxc", [128, 8, 512], F32)
                    mos = [fw.sb("lmo%d" % i, [128, 8, 512], BF16) for i in range(2)]
                    cvw = CV2.rearrange("(j p) t -> p j t", p=128)
                    miw = MI2.rearrange("(j p) t -> p j t", p=128)
                    for ti, (p0s, p0, n) in enumerate([(16, 16, 256), (288, 288, 256), (575, 560, 512), (1087, 1072, 512), (1599, 1584, 2)]):
                        cvt = cvts[ti % 2]
                        mo = mos[ti % 2]
                        fw.load("sp", cvt, cvt[:, :, 0:n], cvw[:, :, p0s:p0s + n])
                        fw.op("act", lambda: nc.scalar.activation(out=sq[:, :, 0:n], in_=cvt[:, :, 0:n], func=AF.Square), r=[cvt], w=[sq])
                        ps1 = fw.psum()
                        ps2 = fw.psum()
                        fw.mm(ps1, ps1[:, 0:n], [(ones_bf[:], cvt[:, j, 0:n]) for j in range(8)], [ones_bf, cvt])
                        fw.mm(ps2, ps2[:, 0:n], [(ones_bf[:], sq[:, j, 0:n]) for j in range(8)], [ones_bf, sq])
                        fw.op("act", lambda: nc.scalar.mul(out=mean[:, 0:n], in_=ps1[:, 0:n], mul=1.0 / 1024), r=[ps1], w=[mean])
                        fw.op("dve", lambda: nc.vector.tensor_tensor(out=msq[:, 0:n], in0=mean[:, 0:n], in1=mean[:, 0:n], op=ALU.mult), r=[mean], w=[msq])
                        fw.op("dve", lambda: nc.vector.scalar_tensor_tensor(out=var[:, 0:n], in0=ps2[:, 0:n], scalar=1.0 / 1024, in1=msq[:, 0:n], op0=ALU.mult, op1=ALU.subtract), r=[ps2, msq], w=[var])
                        fw.op("act", lambda: nc.scalar.activation(out=var[:, 0:n], in_=var[:, 0:n], func=AF.Sqrt, bias=eps_t[:, 0:1], scale=1.0), r=[var, eps_t], w=[var])
                        fw.op("dve", lambda: nc.vector.reciprocal(out=var[:, 0:n], in_=var[:, 0:n]), r=[var], w=[var])
                        fw.op("dve", lambda: nc.vector.tensor_tensor(out=xc[:, :, 0:n], in0=cvt[:, :, 0:n], in1=mean[:, 0:n].unsqueeze(1).to_broadcast([128, 8, n]), op=ALU.subtract), r=[cvt, mean], w=[xc])
                        fw.op("dve", lambda: nc.vector.tensor_tensor(out=xc[:, :, 0:n], in0=xc[:, :, 0:n], in1=var[:, 0:n].unsqueeze(1).to_broadcast([128, 8, n]), op=ALU.mult), r=[xc, var], w=[xc])
                        for j in range(8):
                            fw.op("act", lambda: nc.scalar.activation(out=mo[:, j, 0:n], in_=xc[:, j, 0:n], func=AF.Silu, bias=cvv[:, 2, j:j + 1], scale=cvv[:, 1, j:j + 1]), r=[xc, cvv], w=[mo])
                        fw.store("sp", mo, miw[:, 0:8, p0:p0 + n], mo[:, :, 0:n])

            def hyshort_phase():
                with fw.scope():
                    hw = fw.sb("hw", [128, 24, 3], F32)
                    hb = fw.sb("hb", [128, 24], F32)
                    fw.load("sp", hw, hw[:], hy_sw)
                    fw.load("sp", hb, hb[:], hy_sb)
                    rws = [[fw.sb("hr%d_%d" % (i, q), [128, TP], BF16) for q in range(3)] for i in range(2)]
                    cs_ = [fw.sb("hc%d" % q, [128, TP], F32) for q in range(3)]
                    uo = [fw.sb("huo%d" % i, [128, TP], BF16) for i in range(2)]
                    xo = [fw.sb("hxo%d" % i, [128, TP], BF16) for i in range(2)]
                    for o in uo + xo:
                        fw.op("pool", lambda: nc.gpsimd.memset(o[:], 0.0), w=[o])

                    def conv(e, row, ch, out):
                        eo = nc.vector if e == "dve" else nc.gpsimd
                        fw.op(e, lambda: eo.tensor_scalar(out=out[:, 1:TP - 1], in0=row[:, 0:TP - 2], scalar1=hw[:, ch, 0:1], scalar2=hb[:, ch:ch + 1], op0=ALU.mult, op1=ALU.add), r=[row, hw, hb], w=[out])
                        fw.op(e, lambda: eo.scalar_tensor_tensor(out=out[:, 1:TP - 1], in0=row[:, 1:TP - 1], scalar=hw[:, ch, 1:2], in1=out[:, 1:TP - 1], op0=ALU.mult, op1=ALU.add), r=[row, hw, out], w=[out])
                        fw.op(e, lambda: eo.scalar_tensor_tensor(out=out[:, 1:TP - 1], in0=row[:, 2:TP], scalar=hw[:, ch, 2:3], in1=out[:, 1:TP - 1], op0=ALU.mult, op1=ALU.add), r=[row, hw, out], w=[out])
                    for j in range(8):
                        r0_, r1_, rv_ = rws[j % 2]
                        fw.load("sp", r0_, r0_[:], PJ[(16 + j) * 128:(17 + j) * 128, :])
                        fw.load("sp", r1_, r1_[:], PJ[(24 + j) * 128:(25 + j) * 128, :])
                        fw.load("sp", rv_, rv_[:], PJ[(32 + j) * 128:(33 + j) * 128, :])
                        conv("dve", r0_, j, cs_[0])
                        conv("dve", r1_, 8 + j, cs_[1])
                        conv("dve", rv_, 16 + j, cs_[2])
                        fw.op("dve", lambda: nc.vector.tensor_tensor(out=uo[j % 2][:, 1:TP - 1], in0=cs_[1][:, 1:TP - 1], in1=cs_[2][:, 1:TP - 1], op=ALU.mult), r=[cs_[1], cs_[2]], w=[uo[j % 2]])
                        fw.op("act", lambda: nc.scalar.copy(out=xo[j % 2][:, 1:TP - 1], in_=cs_[0][:, 1:TP - 1]), r=[cs_[0]], w=[xo[j % 2]])
                        fw.store("sp", uo[j % 2], UU[j * 128:(j + 1) * 128, :], uo[j % 2][:])
                        fw.store("sp", xo[j % 2], X0[j * 128:(j + 1) * 128, :], xo[j % 2][:])

            def hyena_phase(L, seqs):
                C = HYC[L]
                TC = L // 128
                FC = TC
                NB = min(512, L)
                NBN = L // NB
                HS = C["HS"]
                with fw.scope():
                    hs = fw.sb("hs", [128, TC, 1024], BF16)
                    hd = fw.sb("hd", [128, TC, 1024], BF16)
                    with fw.scope():
                        w1t = fw.sb("w1t", [33, 64], F32)
                        w2t = fw.sb("w2t", [64, 64], F32)
                        w3t = fw.sb("w3t", [64, 2048], F32)
                        hv = fw.sb("hv", [64, 4], F32)
                        sc = fw.sb("hsc", [64, 8], F32)
                        zt = fw.sb("zt", [33, L], F32)
                        hid2 = fw.sb("hid2", [64, L], F32)
                        negt = fw.sb("negt", [128, TC], F32)
                        adl = fw.sb("adl", [128, 1024], F32)
                        for (t_, d_) in ((w1t, hy_w1), (w2t, hy_w2), (w3t, hy_w3), (hv, hy_v), (zt, C["zT"]), (negt, C["negt"]), (adl, absdel)):
                            fw.load("sp", t_, t_[:], d_)
                        for li in range(2):
                            fcol = hv[:, 2 * li + 1:2 * li + 2]
                            bcol = hv[:, 2 * li:2 * li + 1]
                            fw.op("dve", lambda: nc.vector.tensor_scalar(out=sc[:, 4 * li:4 * li + 1], in0=fcol, scalar1=0.25, scalar2=None, op0=ALU.mult), r=[hv], w=[sc])
                            fw.op("dve", lambda: nc.vector.scalar_tensor_tensor(out=sc[:, 4 * li + 1:4 * li + 2], in0=fcol, scalar=0.25, in1=bcol, op0=ALU.mult, op1=ALU.mult), r=[hv], w=[sc])
                            fw.op("dve", lambda: nc.vector.tensor_scalar(out=sc[:, 4 * li + 2:4 * li + 3], in0=fcol, scalar1=0.125, scalar2=None, op0=ALU.mult), r=[hv], w=[sc])
                            fw.op("dve", lambda: nc.vector.scalar_tensor_tensor(out=sc[:, 4 * li + 3:4 * li + 4], in0=fcol, scalar=0.125, in1=bcol, op0=ALU.mult, op1=ALU.mult), r=[hv], w=[sc])
                        s4 = fw.sb("s4", [64, 512], F32)
                        s8 = fw.sb("s8", [64, 512], F32)
                        tq = fw.sb("tq", [64, 512], F32)
                        c4 = fw.sb("c4", [64, 512], F32)
                        h1b = fw.sb("h1b", [64, 512], F32)

                        def sin_big(ps, li, out_ap, n, wb):
                            fw.op("act", lambda: nc.scalar.activation(out=s4[:, 0:n], in_=ps[0:64, 0:n], func=AF.Sin, bias=sc[:, 4 * li + 1:4 * li + 2], scale=sc[:, 4 * li:4 * li + 1]), r=[ps, sc], w=[s4])
                            fw.op("act", lambda: nc.scalar.activation(out=s8[:, 0:n], in_=ps[0:64, 0:n], func=AF.Sin, bias=sc[:, 4 * li + 3:4 * li + 4], scale=sc[:, 4 * li + 2:4 * li + 3]), r=[ps, sc], w=[s8])
                            fw.op("dve", lambda: nc.vector.tensor_tensor(out=tq[:, 0:n], in0=s8[:, 0:n], in1=s8[:, 0:n], op=ALU.mult), r=[s8], w=[tq])
                            fw.op("dve", lambda: nc.vector.tensor_scalar(out=c4[:, 0:n], in0=tq[:, 0:n], scalar1=-2.0, scalar2=1.0, op0=ALU.mult, op1=ALU.add), r=[tq], w=[c4])
                            fw.op("dve", lambda: nc.vector.scalar_tensor_tensor(out=c4[:, 0:n], in0=s4[:, 0:n], scalar=2.0, in1=c4[:, 0:n], op0=ALU.mult, op1=ALU.mult), r=[s4, c4], w=[c4])
                            fw.op("dve", lambda: nc.vector.tensor_tensor(out=tq[:, 0:n], in0=s4[:, 0:n], in1=s4[:, 0:n], op=ALU.mult), r=[s4], w=[tq])
                            fw.op("dve", lambda: nc.vector.tensor_scalar(out=tq[:, 0:n], in0=tq[:, 0:n], scalar1=-2.0, scalar2=1.0, op0=ALU.mult, op1=ALU.add), r=[tq], w=[tq])
                            fw.op("dve", lambda: nc.vector.scalar_tensor_tensor(out=out_ap, in0=c4[:, 0:n], scalar=2.0, in1=tq[:, 0:n], op0=ALU.mult, op1=ALU.mult), r=[c4, tq], w=[wb])
                        for blk in range(L // NB):
                            n = NB
                            ps = fw.psum()
                            fw.mm(ps, ps[0:64, 0:n], [(w1t[0:33, :], zt[0:33, blk * NB:(blk + 1) * NB])], [w1t, zt])
                            sin_big(ps, 0, h1b[:, 0:n], n, h1b)
                            ps2 = fw.psum()
                            fw.mm(ps2, ps2[0:64, 0:n], [(w2t[:, :], h1b[:, 0:n])], [w2t, h1b])
                            sin_big(ps2, 1, hid2[:, blk * NB:(blk + 1) * NB], n, hid2)
                        dk = fw.sb("dk", [128, 1024], F32)
                        fd = fw.sb("fd", [128, 1024], F32)
                        bd = fw.sb("bd", [128, 1024], F32)
                        for tb in range(TC):
                            pbs = [fw.psum() for _ in range(4)]
                            for c in range(4):
                                fw.mm(pbs[c], pbs[c][:, :], [(hid2[0:64, tb * 128:(tb + 1) * 128], w3t[0:64, c * 512:(c + 1) * 512])], [hid2, w3t])
                            fw.op("act", lambda: nc.scalar.activation(out=dk[:], in_=adl[:], func=AF.Exp, scale=negt[:, tb:tb + 1]), r=[adl, negt], w=[dk])
                            for c in range(2):
                                fw.op("dve", lambda: nc.vector.tensor_tensor(out=fd[:, c * 512:(c + 1) * 512], in0=pbs[c][:, :], in1=dk[:, c * 512:(c + 1) * 512], op=ALU.mult), r=[pbs[c], dk], w=[fd])
                                fw.op("dve", lambda: nc.vector.tensor_tensor(out=bd[:, c * 512:(c + 1) * 512], in0=pbs[2 + c][:, :], in1=dk[:, c * 512:(c + 1) * 512], op=ALU.mult), r=[pbs[2 + c], dk], w=[bd])
                            if tb == 0:
                                fw.op("dve", lambda: nc.vector.memset(bd[0:1, :], 0.0), w=[bd])
                            fw.op("pool", lambda: nc.gpsimd.tensor_tensor(out=hs[:, tb, :], in0=fd[:], in1=bd[:], op=ALU.add), r=[fd, bd], w=[hs])
                            fw.op("pool", lambda: nc.gpsimd.tensor_tensor(out=hd[:, tb, :], in0=fd[:], in1=bd[:], op=ALU.subtract), r=[fd, bd], w=[hd])
                    with fw.scope():
                        fcts = [fw.sb("fct%d" % i, [128, TC, 128], BF16) for i in range(2)]
                        fsts = [fw.sb("fst%d" % i, [128, TC, 128], BF16) for i in range(2)]
                        hsts = [fw.sb("hst%d" % i, [128, 2, 1024], F32) for i in range(2)]
                        for fc in range(FC):
                            fct = fcts[fc % 2]
                            fst = fsts[fc % 2]
                            hst = hsts[fc % 2]
                            fw.load("sp", fct, fct[:], C["fwc"][fc])
                            fw.load("sp", fst, fst[:], C["fws"][fc])
                            for ri, (tab, hx) in enumerate(((fct, hs), (fst, hd))):
                                for h in range(2):
                                    pb = fw.psum()
                                    fw.mm(pb, pb[:, :], [(tab[:, tc, :], hx[:, tc, h * 512:(h + 1) * 512]) for tc in range(TC)], [tab, hx])
                                    evac(hst[:, ri, h * 512:(h + 1) * 512], pb[:, :], [pb], [hst], fc)
                            fw.store("sp", hst, HS[:, fc * 128:(fc + 1) * 128, :].rearrange("r p c -> p r c"), hst[:])
                for s0 in seqs:
                    sp0 = ppos(s0)
                    with fw.scope():
                        ut = fw.sb("ut", [128, TC, 1024], BF16)
                        with fw.scope():
                            idt = fw.sb("idt", [128, 128], F32)
                            fw.load("sp", idt, idt[:], identf)
                            urs = [fw.sb("ur%d" % i, [128, L], F32) for i in range(2)]
                            for j in range(8):
                                ur = urs[j % 2]
                                fw.load("pool", ur, ur[:], UU[j * 128:(j + 1) * 128, sp0:sp0 + L])
                                nq = min(4, TC)
                                for g in range(TC // nq):
                                    pb = fw.psum()
                                    for q in range(nq):
                                        fw.transpose(pb, pb[:, q * 128:(q + 1) * 128], ur[:, (g * nq + q) * 128:(g * nq + q + 1) * 128], idt[:], [ur, idt])
                                    evac(ut[:, g * nq:(g + 1) * nq, j * 128:(j + 1) * 128], pb[:, 0:nq * 128].rearrange("p (q c) -> p q c", c=128), [pb], [ut], 0)
                        fcts = [fw.sb("gct%d" % i, [128, TC, 128], BF16) for i in range(2)]
                        fsts = [fw.sb("gst%d" % i, [128, TC, 128], BF16) for i in range(2)]
                        hts = [fw.sb("hts%d" % i, [128, 2, 1024], F32) for i in range(2)]
                        ysts = [fw.sb("yst%d" % i, [128, 2, 1024], BF16) for i in range(2)]
                        t1 = fw.sb("t1", [128, 512], F32)
                        t2 = fw.sb("t2", [128, 512], F32)
                        t3 = fw.sb("t3", [128, 512], F32)
                        t4 = fw.sb("t4", [128, 512], F32)
                        for fc in range(FC):
                            fct = fcts[fc % 2]
                            fst = fsts[fc % 2]
                            ht = hts[fc % 2]
                            yst = ysts[fc % 2]
                            fw.load("sp", fct, fct[:], C["fwc"][fc])
                            fw.load("sp", fst, fst[:], C["fws"][fc])
                            fw.load("sp", ht, ht[:], HS[:, fc * 128:(fc + 1) * 128, :].rearrange("r p c -> p r c"))
                            for h in range(2):
                                hsl = slice(h * 512, (h + 1) * 512)
                                pr = fw.psum()
                                pi = fw.psum()
                                fw.mm(pr, pr[:, :], [(fct[:, tc, :], ut[:, tc, hsl]) for tc in range(TC)], [fct, ut])
                                fw.mm(pi, pi[:, :], [(fst[:, tc, :], ut[:, tc, hsl]) for tc in range(TC)], [fst, ut])
                                fw.op("dve", lambda: nc.vector.tensor_tensor(out=t1[:], in0=pr[:, :], in1=ht[:, 0, hsl], op=ALU.mult), r=[pr, ht], w=[t1])
                                fw.op("dve", lambda: nc.vector.tensor_tensor(out=t3[:], in0=pr[:, :], in1=ht[:, 1, hsl], op=ALU.mult), r=[pr, ht], w=[t3])
                                fw.op("dve", lambda: nc.vector.tensor_tensor(out=t2[:], in0=pi[:, :], in1=ht[:, 1, hsl], op=ALU.mult), r=[pi, ht], w=[t2])
                                fw.op("dve", lambda: nc.vector.tensor_tensor(out=t4[:], in0=pi[:, :], in1=ht[:, 0, hsl], op=ALU.mult), r=[pi, ht], w=[t4])
                                fw.op("pool", lambda: nc.gpsimd.tensor_tensor(out=yst[:, 0, hsl], in0=t1[:], in1=t2[:], op=ALU.subtract), r=[t1, t2], w=[yst])
                                fw.op("pool", lambda: nc.gpsimd.tensor_tensor(out=yst[:, 1, hsl], in0=t3[:], in1=t4[:], op=ALU.add), r=[t3, t4], w=[yst])
                            for ri in range(2):
                                fw.store("sp", yst, YD[:, ri, :, fc, :].rearrange("j p c -> p j c"), yst[:, ri, :].rearrange("p (j c) -> p j c", c=128))
                    if L == 4096:
                        with fw.scope():
                            hbt = fw.sb("hbt", [128, 8], F32)
                            qs = fw.sb("hqs", [128, 4], F32)
                            fw.load("sp", hbt, hbt[:], hy_bias)
                            fw.load("sp", qs, qs[:], qsel)
                            x0q = fw.sb("x0q", [128, 8, 1026], BF16)
                            uq = fw.sb("uq", [128, 8, 1026], BF16)
                            with fw.scope():
                                frs = [fw.sb("frow%d" % i, [128, TP], BF16) for i in range(2)]
                                k = 0
                                for j in range(8):
                                    for (srcT, dstq) in ((X0, x0q), (UU, uq)):
                                        fr = frs[k % 2]
                                        k += 1
                                        fw.load("sp", fr, fr[:], srcT[j * 128:(j + 1) * 128, :])
                                        fw.op("dve", lambda: nc.vector.tensor_scalar(out=dstq[:, j, :], in0=fr[:, 559:559 + 1026], scalar1=qs[:, 0:1], scalar2=None, op0=ALU.mult), r=[fr, qs], w=[dstq])
                                        for q in range(1, 4):
                                            fw.op("dve", lambda: nc.vector.scalar_tensor_tensor(out=dstq[:, j, :], in0=fr[:, 559 + q * 1024:559 + q * 1024 + 1026], scalar=qs[:, q:q + 1], in1=dstq[:, j, :], op0=ALU.mult, op1=ALU.add), r=[fr, qs, dstq], w=[dstq])
                            gct = fw.sb("qct", [128, FC, 512], BF16)
                            gst = fw.sb("qst", [128, FC, 512], BF16)
                            ght = fw.sb("qht", [128, 2, FC, 2], BF16)
                            fw.load("sp", ght, ght[:], ivh.rearrange("r p f n -> p r f n"))
                            yrts = [fw.sb("yrt%d" % i, [128, FC, 128], BF16) for i in range(2)]
                            yits = [fw.sb("yit%d" % i, [128, FC, 128], BF16) for i in range(2)]
                            ubs = [fw.sb("ub%d" % i, [128, 512], F32) for i in range(2)]
                            zzs = [fw.sb("zz%d" % i, [128, 512], BF16) for i in range(2)]
                            k = 0
                            blocks = [(0, 512, [(1, 561, 0, 512)]), (1, 512, [(513, 1073, 0, 512)]), (None, 2, [(0, 560, 0, 1), (1025, 1585, 1, 1)])]
                            for (bi, n, parts) in blocks:
                                if bi is not None:
                                    fw.load("sp", gct, gct[:], ivcq[bi])
                                    fw.load("sp", gst, gst[:], ivsq[bi])
                                for j in range(8):
                                    yrt, yit, ub, zz = yrts[k % 2], yits[k % 2], ubs[k % 2], zzs[k % 2]
                                    k += 1
                                    fw.load("sp", yrt, yrt[:], YD[j, 0])
                                    fw.load("sp", yit, yit[:], YD[j, 1])
                                    pb = fw.psum()
                                    if bi is not None:
                                        pairs = [(yrt[:, fc, :], gct[:, fc, :]) for fc in range(FC)] + [(yit[:, fc, :], gst[:, fc, :]) for fc in range(FC)]
                                        rd = [yrt, yit, gct, gst]
                                    else:
                                        pairs = [(yrt[:, fc, :], ght[:, 0, fc, :]) for fc in range(FC)] + [(yit[:, fc, :], ght[:, 1, fc, :]) for fc in range(FC)]
                                        rd = [yrt, yit, ght]
                                    fw.mm(pb, pb[:, 0:n], pairs, rd)
                                    for (sc, dc, pc, wd) in parts:
                                        fw.op("dve", lambda: nc.vector.tensor_scalar(out=ub[:, pc:pc + wd], in0=uq[:, j, sc:sc + wd], scalar1=hbt[:, j:j + 1], scalar2=None, op0=ALU.mult), r=[uq, hbt], w=[ub])
                                        fw.op("dve", lambda: nc.vector.scalar_tensor_tensor(out=ub[:, pc:pc + wd], in0=pb[:, pc:pc + wd], scalar=1.0 / L, in1=ub[:, pc:pc + wd], op0=ALU.mult, op1=ALU.add), r=[pb, ub], w=[ub])
                                        fw.op("pool", lambda: nc.gpsimd.tensor_tensor(out=zz[:, pc:pc + wd], in0=ub[:, pc:pc + wd], in1=x0q[:, j, sc:sc + wd], op=ALU.mult), r=[ub, x0q], w=[zz])
                                        fw.store("sp", zz, MI2[1024 + j * 128:1024 + (j + 1) * 128, dc:dc + wd], zz[:, pc:pc + wd])
                        continue
                    with fw.scope():
                        hbt = fw.sb("hbt", [128, 8], F32)
                        fw.load("sp", hbt, hbt[:], hy_bias)
                        gcts = [fw.sb("ict%d" % i, [128, FC, NB], BF16) for i in range(2)]
                        gsts = [fw.sb("ist%d" % i, [128, FC, NB], BF16) for i in range(2)]
                        yrts = [fw.sb("yrt%d" % i, [128, FC, 128], BF16) for i in range(2)]
                        yits = [fw.sb("yit%d" % i, [128, FC, 128], BF16) for i in range(2)]
                        x0ts = [fw.sb("x0t%d" % i, [128, NB], BF16) for i in range(2)]
                        u2ts = [fw.sb("u2t%d" % i, [128, NB], BF16) for i in range(2)]
                        ubs = [fw.sb("ub%d" % i, [128, NB], F32) for i in range(2)]
                        zzs = [fw.sb("zz%d" % i, [128, NB], BF16) for i in range(2)]
                        k = 0
                        for nb in range(NBN):
                            gct = gcts[nb % 2]
                            gst = gsts[nb % 2]
                            fw.load("sp", gct, gct[:], C["ivc"][nb])
                            fw.load("sp", gst, gst[:], C["ivs"][nb])
                            c0 = sp0 + nb * NB
                            for j in range(8):
                                yrt, yit, x0t, u2t, ub, zz = yrts[k % 2], yits[k % 2], x0ts[k % 2], u2ts[k % 2], ubs[k % 2], zzs[k % 2]
                                k += 1
                                fw.load("sp", yrt, yrt[:, 0:FC, :], YD[j, 0, :, 0:FC, :])
                                fw.load("sp", yit, yit[:, 0:FC, :], YD[j, 1, :, 0:FC, :])
                                fw.load("sp", x0t, x0t[:], X0[j * 128:(j + 1) * 128, c0:c0 + NB])
                                fw.load("sp", u2t, u2t[:], UU[j * 128:(j + 1) * 128, c0:c0 + NB])
                                pb = fw.psum()
                                fw.mm(pb, pb[:, 0:NB], [(yrt[:, fc, :], gct[:, fc, :]) for fc in range(FC)] + [(yit[:, fc, :], gst[:, fc, :]) for fc in range(FC)], [yrt, yit, gct, gst])
                                fw.op("dve", lambda: nc.vector.tensor_scalar(out=ub[:], in0=u2t[:], scalar1=hbt[:, j:j + 1], scalar2=None, op0=ALU.mult), r=[u2t, hbt], w=[ub])
                                fw.op("dve", lambda: nc.vector.scalar_tensor_tensor(out=ub[:], in0=pb[:, 0:NB], scalar=1.0 / L, in1=ub[:], op0=ALU.mult, op1=ALU.add), r=[pb, ub], w=[ub])
                                fw.op("pool", lambda: nc.gpsimd.tensor_tensor(out=zz[:], in0=ub[:], in1=x0t[:], op=ALU.mult), r=[ub, x0t], w=[zz])
                                fw.store("sp", zz, MI2[1024 + j * 128:1024 + (j + 1) * 128, c0:c0 + NB], zz[:])

            def odd_mixer_phase(x_src):
                with fw.scope():
                    hT = fw.sb("hT1", [128, 16, T], BF16)
                    norm_phase(x_src, modA1[1], modB1[1], hT)
                    with fw.scope():
                        wts = [fw.sb("wq%d" % i, [128, 16, 512], BF16) for i in range(2)]
                        rows = make_rows(2)
                        proj_rows(hT, 16, o_w_in, 0, 5120, PJ, 0, wts, rows)
                if upto >= 6.2:
                    conformer_phase()
                if upto >= 6.4:
                    hyshort_phase()
                if upto >= 6.6:
                    hyena_phase(256, [0, 256])
                if upto >= 6.8:
                    hyena_phase(4096, [512])

            if upto >= 3:
                attention_phase()
            if upto >= 4:
                wout_phase(e_w_out, xT, xA if upto >= 5 else yT, modG1[0])
            if upto >= 5:
                ffn_phase(0, xA, xB if upto >= 6 else yT)
            if upto >= 6:
                odd_mixer_phase(xB)
            if 7 <= upto < 8:
                wout_phase(o_w_out, xB, yT, modG1[1])
            if upto >= 8:
                T2 = GEO_TAIL["T"]
                with fw.scope():
                    mTq = fw.sb("mTq", [128, 16, T2], BF16)
                    with fw.scope():
                        qs = fw.sb("qs", [128, 4], F32)
                        fw.load("sp", qs, qs[:], qsel)
                        set_geo(GEO_TAIL)
                        load_act_from(MI2, mTq)
                        set_geo(GEO_FULL)
                        xrs = [fw.sb("xrow%d" % i, [128, 4609], F32) for i in range(2)]
                        xqs = [fw.sb("xq%d" % i, [128, T2], F32) for i in range(2)]
                        for xr_ in xrs:
                            fw.op("pool", lambda: nc.gpsimd.memset(xr_[:, 4608:4609], 0.0), w=[xr_])
                        for kc in range(16):
                            xr_ = xrs[kc % 2]
                            xq_ = xqs[kc % 2]
                            fw.load("sp", xr_, xr_[:, 0:4608], xB[kc * 128:(kc + 1) * 128, :])
                            fw.op("act", lambda: nc.scalar.copy(out=xq_[:, 0:512], in_=xr_[:, 0:512]), r=[xr_], w=[xq_])
                            fw.op("dve", lambda: nc.vector.tensor_scalar(out=xq_[:, 512:T2], in0=xr_[:, 511:511 + 1026], scalar1=qs[:, 0:1], scalar2=None, op0=ALU.mult), r=[xr_, qs], w=[xq_])
                            for q in range(1, 4):
                                fw.op("dve", lambda: nc.vector.scalar_tensor_tensor(out=xq_[:, 512:T2], in0=xr_[:, 511 + q * 1024:511 + q * 1024 + 1026], scalar=qs[:, q:q + 1], in1=xq_[:, 512:T2], op0=ALU.mult, op1=ALU.add), r=[xr_, qs, xq_], w=[xq_])
                            fw.store("sp", xq_, xQ[kc * 128:(kc + 1) * 128, :], xq_[:])
                    set_geo(GEO_TAIL)
                    with fw.scope():
                        wts = [fw.sb("wo2_%d" % i, [128, 16, 512], BF16) for i in range(2)]
                        proj_resid(mTq, 16, o_w_out, xQ, xA2, modG1[1], wts)
                with fw.scope():
                    hm = fw.sb("hm", [128, 2], F32)
                    fw.load("sp", hm, hm[:], hmask)
                    ffn_phase(1, xA2, yQ, PJ2, FF2, hm)
                set_geo(GEO_FULL)

            fw.barrier()
    return nc


def _chunks(vec, n):
    return np.ascontiguousarray(np.asarray(vec, np.float32).reshape(n, 128).T)


_CONST_CACHE = {}


def _consts():
    if _CONST_CACHE:
        return _CONST_CACHE
    f = np.float32
    out = {}
    max_decay = math.log(1e-2) / 0.3
    min_decay = math.log(1e-2) / 1.5
    deltas = np.linspace(min_decay, max_decay, 1024, dtype=f)
    out["absdel"] = np.ascontiguousarray(np.broadcast_to(np.abs(deltas)[None, :], (128, 1024)), dtype=f)
    out["identf"] = np.eye(128, dtype=f)
    for L in (256, 4096):
        t = np.linspace(0.0, 1.0, L, dtype=f)
        ang = (2 * math.pi * np.arange(L, dtype=f) / L).astype(f)
        freqs = np.linspace(1e-4, 15, 16, dtype=f)
        z = np.concatenate([t[:, None], np.cos(freqs[None, :] * ang[:, None]), -np.sin(freqs[None, :] * ang[:, None])], axis=-1).astype(f)
        out["zT%d" % L] = np.ascontiguousarray(z.T)
        out["negt%d" % L] = np.ascontiguousarray((-t).reshape(L // 128, 128).T, dtype=f)
        tt = np.arange(L, dtype=np.int64)[:, None]
        ff = np.arange(L, dtype=np.int64)[None, :]
        ph = ((2 * ff + 1) * tt) % (4 * L)
        angm = ph.astype(np.float64) * (2 * math.pi / (4 * L))
        TCn = L // 128
        NB = min(512, L)
        for nm, M in (("c", np.cos(angm)), ("s", np.sin(angm))):
            Mb = M.astype(ml_dtypes.bfloat16)
            out["fw%s%d" % (nm, L)] = np.ascontiguousarray(Mb.reshape(TCn, 128, TCn, 128).transpose(2, 1, 0, 3))
            out["iv%s%d" % (nm, L)] = np.ascontiguousarray(Mb.reshape(L // NB, NB, TCn, 128).transpose(0, 3, 2, 1))
    _CONST_CACHE.update(out)
    return _CONST_CACHE


def prep_inputs(inp, core):
    f = np.float32
    sbi = core // 4
    p0, p1 = 2 * core, 2 * core + 1
    xp, xs = inp["x_prompt"], inp["x_sample"]
    xT = np.ascontiguousarray(np.concatenate([xp[p0].T, xp[p1].T, xs[sbi].T], axis=1), dtype=f)
    cs = np.stack([inp["c_ctx"], inp["c"][sbi]], axis=-1)
    csil = np.ascontiguousarray(cs.reshape(16, 128, 2).transpose(1, 0, 2).reshape(128, 32), dtype=f)
    adab = np.ascontiguousarray(inp["ada_b"].reshape(2, 96, 128).transpose(2, 0, 1), dtype=f)
    nmix = np.ascontiguousarray(inp["norm_mix"].reshape(2, 16, 128).transpose(2, 0, 1), dtype=f)
    nffn = np.ascontiguousarray(inp["norm_ffn"].reshape(2, 16, 128).transpose(2, 0, 1), dtype=f)
    conv_a = np.ascontiguousarray(inp["e_conv_a"][0].reshape(3, 8, 128).transpose(2, 1, 0), dtype=f)
    qkn = np.ascontiguousarray(np.stack([inp["e_q_norm"][0], inp["e_k_norm"][0]], axis=-1), dtype=f)
    rpbp = np.zeros((8, 15, 127), f)
    rpbp[:, :, 48:79] = inp["e_rpb"][0]
    ckT = np.ascontiguousarray(inp["cache_k"][sbi, 0].transpose(1, 2, 0), dtype=f)
    cv = np.ascontiguousarray(inp["cache_v"][sbi, 0].reshape(512, 1024), dtype=f)
    ffn_conv = np.ascontiguousarray(inp["ffn_conv"].reshape(2, 3, 44, 128).transpose(3, 0, 2, 1), dtype=f)
    j64 = np.ascontiguousarray(np.eye(64, dtype=f)[::-1])
    kc = np.arange(64)[:, None]
    qc = np.arange(64)[None, :]
    cst = np.clip(qc - 8, 0, 48)
    cm = ((kc >= cst) & (kc < cst + 16)).astype(f)
    colmask = np.ascontiguousarray(np.concatenate([cm, cm], axis=0))
    extra = dict(_consts())
    extra["conf_dw"] = np.ascontiguousarray(inp["o_conf_dw"][0].reshape(31, 8, 128).transpose(2, 1, 0), dtype=f)
    extra["conf_v"] = np.ascontiguousarray(np.stack([inp["o_conf_dw_b"][0], inp["o_conf_ln_g"][0], inp["o_conf_ln_b"][0]], 0).reshape(3, 8, 128).transpose(2, 0, 1), dtype=f)
    extra["hy_sw"] = np.ascontiguousarray(inp["o_hy_short"][0].reshape(3, 24, 128).transpose(2, 1, 0), dtype=f)
    extra["hy_sb"] = _chunks(inp["o_hy_short_b"][0], 24)
    extra["hy_w1"] = np.ascontiguousarray(inp["o_hy_w1"][0], dtype=f)
    extra["hy_w2"] = np.ascontiguousarray(inp["o_hy_w2"][0], dtype=f)
    extra["hy_w3"] = np.ascontiguousarray(inp["o_hy_w3"][0], dtype=f)
    extra["hy_v"] = np.ascontiguousarray(np.stack([inp["o_hy_b1"][0], inp["o_hy_f1"][0], inp["o_hy_b2"][0], inp["o_hy_f2"][0]], -1), dtype=f)
    extra["hy_bias"] = _chunks(inp["o_hy_bias"][0], 8)
    r_ = core % 4
    cst = _consts()
    extra["ivcq"] = np.ascontiguousarray(cst["ivc4096"][2 * r_:2 * r_ + 2])
    extra["ivsq"] = np.ascontiguousarray(cst["ivs4096"][2 * r_:2 * r_ + 2])
    hl = max(r_ * 1024 - 1, 0)
    hr = min((r_ + 1) * 1024, 4095)
    ivh_ = np.zeros((2, 128, 32, 2), ml_dtypes.bfloat16)
    for ci_, nm_ in enumerate(("ivc4096", "ivs4096")):
        for hi_, n_ in enumerate((hl, hr)):
            ivh_[ci_, :, :, hi_] = cst[nm_][n_ // 512, :, :, n_ % 512]
    extra["ivh"] = ivh_
    extra["qsel"] = np.ascontiguousarray(np.broadcast_to(np.eye(4, dtype=f)[r_][None, :], (128, 4)))
    extra["hmask"] = np.ascontiguousarray(np.broadcast_to(np.array([0.0 if r_ == 0 else 1.0, 0.0 if r_ == 3 else 1.0], f)[None, :], (128, 2)))
    return {
        **extra,
        "xT": xT, "csil": csil, "ada_w": inp["ada_w"], "adab": adab, "nmix": nmix, "nffn": nffn,
        "e_w_in": inp["e_w_in"][0], "e_w_out": inp["e_w_out"][0], "o_w_in": inp["o_w_in"][0], "o_w_out": inp["o_w_out"][0],
        "ffn_in": inp["ffn_in"], "ffn_out": inp["ffn_out"], "conv_a": conv_a, "qkn": qkn, "rpbp": rpbp,
        "ckT": ckT, "cv": cv, "ffn_conv": ffn_conv, "j64": j64, "colmask": colmask,
    }


def kernel(**inputs):
    inp = {k: np.asarray(v) for k, v in inputs.items()}
    nc = build()
    in_maps = [prep_inputs(inp, c) for c in range(NCORES)]
    res = run_bass_kernel_spmd(nc, in_maps, core_ids=list(range(NCORES)))
    R = res.results
    y_prompt = np.zeros((16, 256, D), np.float32)
    y_sample = np.zeros((2, 4096, D), np.float32)
    nk = np.zeros((16, 1, 256, 8, 128), np.float32)
    nv = np.zeros((16, 1, 256, 8, 128), np.float32)
    for c in range(NCORES):
        yQ = R[c]["yQ"]
        for s in range(2):
            y_prompt[2 * c + s] = yQ[:, s * 256:(s + 1) * 256].T
            nk[2 * c + s, 0] = R[c]["kT_out"][:, s * 256:(s + 1) * 256].T.reshape(256, 8, 128)
            nv[2 * c + s, 0] = R[c]["v_out"][s * 256:(s + 1) * 256].reshape(256, 8, 128)
        r_ = c % 4
        y_sample[c // 4, r_ * 1024:(r_ + 1) * 1024] = yQ[:, 513:1537].T
    return (y_prompt, y_sample, nk, nv)
```
